# Optimizing a Trainium2 kernel written in Bass

```python
import math
import jax, jax.numpy as jnp
from jax import lax
import numpy as np

D_MODEL = 1024
BATCH = 8
SEQ = 2048
DEPTH = 4

N_EVEN = (DEPTH + 1) // 2
N_ODD = DEPTH // 2
EPS = 1e-6

DA_HEADS = 4
DA_HEAD_DIM = 64
DA_V_DIM = 2 * DA_HEAD_DIM
DA_QK_WIDTH = DA_HEADS * 2 * DA_HEAD_DIM
DA_V_WIDTH = DA_HEADS * DA_V_DIM
ROPE_THETA = 10000.0
Q_BLOCK = 128

SG_GROUPS = 4
SG_CHUNK = 128
SG_GROUP_DIM = 128
SG_WIDTH = SG_GROUPS * SG_GROUP_DIM

AB_IN_WIDTH = 2 * DA_QK_WIDTH + DA_V_WIDTH + 2 * SG_WIDTH
AB_OUT_WIDTH = DA_V_WIDTH + SG_WIDTH

CONV_INNER = D_MODEL
CONV_WIDTH = 31

FFN_HIDDEN = 2816
FFN_CONV_WIDTH = 3

kernel_name = "hybrid_diffattn_sgmlp_conformer_convffn"


def rms_norm(x, g):
    x32 = x.astype(jnp.float32)
    y = x32 * lax.rsqrt(jnp.mean(x32 * x32, axis=-1, keepdims=True) + EPS)
    return (y * g.astype(jnp.float32)).astype(x.dtype)


def layer_norm(x, g, b):
    x32 = x.astype(jnp.float32)
    mu = jnp.mean(x32, axis=-1, keepdims=True)
    xc = x32 - mu
    y = xc * lax.rsqrt(jnp.mean(xc * xc, axis=-1, keepdims=True) + EPS)
    return (y * g.astype(jnp.float32) + b.astype(jnp.float32)).astype(x.dtype)


def causal_dwconv(x, w, b):
    k = w.shape[0]
    y = lax.conv_general_dilated(
        x, w[:, None, :].astype(x.dtype), window_strides=(1,),
        padding=[(k - 1, 0)], dimension_numbers=("NWC", "WIO", "NWC"),
        feature_group_count=x.shape[-1])
    return y + b.astype(x.dtype)


def rope_tables(seq, dim):
    inv = 1.0 / (ROPE_THETA ** (jnp.arange(0, dim, 2, dtype=jnp.float32) / dim))
    ang = jnp.arange(seq, dtype=jnp.float32)[:, None] * inv[None, :]
    ang = jnp.concatenate([ang, ang], axis=-1)
    return jnp.cos(ang), jnp.sin(ang)


def apply_rope(x, cos, sin):
    x32 = x.astype(jnp.float32)
    x1, x2 = jnp.split(x32, 2, axis=-1)
    rot = jnp.concatenate([-x2, x1], axis=-1)
    c = cos[None, :, None, None, :]
    s = sin[None, :, None, None, :]
    return (x32 * c + rot * s).astype(x.dtype)


def diff_attention(q, k, v, lam):
    b, s, h, _, d = q.shape
    nb = s // Q_BLOCK
    scale = d ** -0.5
    qb = q.reshape(b, nb, Q_BLOCK, h, 2, d).transpose(1, 0, 2, 3, 4, 5)
    kpos = jnp.arange(s)

    def block(args):
        q_blk, start = args
        sc = jnp.einsum("bqhcd,bkhcd->bhcqk", q_blk, k,
                        preferred_element_type=jnp.float32) * scale
        qpos = start + jnp.arange(Q_BLOCK)
        mask = kpos[None, :] <= qpos[:, None]
        sc = jnp.where(mask, sc, -jnp.inf)
        p = jax.nn.softmax(sc, axis=-1)
        a = p[:, :, 0] - lam * p[:, :, 1]
        return jnp.einsum("bhqk,bkhe->bqhe", a.astype(v.dtype), v)

    out = lax.map(block, (qb, jnp.arange(nb) * Q_BLOCK))
    return out.transpose(1, 0, 2, 3, 4).reshape(b, s, h, v.shape[-1])


def spatial_gating(u, z, ln_g, ln_b, w_s, b_s):
    b, s, _ = u.shape
    nc = s // SG_CHUNK
    z = z.reshape(b, s, SG_GROUPS, SG_GROUP_DIM)
    z = layer_norm(z, ln_g.reshape(SG_GROUPS, SG_GROUP_DIM),
                   ln_b.reshape(SG_GROUPS, SG_GROUP_DIM))
    z = z.reshape(b, nc, SG_CHUNK, SG_GROUPS, SG_GROUP_DIM)
    tri = jnp.tril(jnp.ones((SG_CHUNK, SG_CHUNK), dtype=bool))
    w = jnp.where(tri[None], w_s, 0).astype(z.dtype)
    zs = jnp.einsum("gts,bcsgd->bctgd", w, z) \
        + b_s.T.astype(z.dtype)[None, None, :, :, None]
    return u * zs.reshape(b, s, SG_WIDTH)


def even_mixer(h, layer_idx, w_in, w_out, lq1, lk1, lq2, lk2, subln_g,
               sg_ln_g, sg_ln_b, sg_w, sg_b, cos, sin):
    b, s, _ = h.shape
    proj = h @ w_in
    o0 = DA_QK_WIDTH
    o1 = 2 * DA_QK_WIDTH
    o2 = o1 + DA_V_WIDTH
    q = proj[..., :o0].reshape(b, s, DA_HEADS, 2, DA_HEAD_DIM)
    k = proj[..., o0:o1].reshape(b, s, DA_HEADS, 2, DA_HEAD_DIM)
    v = proj[..., o1:o2].reshape(b, s, DA_HEADS, DA_V_DIM)
    uz = jax.nn.gelu(proj[..., o2:])
    u, z = jnp.split(uz, 2, axis=-1)

    q = apply_rope(q, cos, sin)
    k = apply_rope(k, cos, sin)
    lambda_init = 0.8 - 0.6 * math.exp(-0.3 * layer_idx)
    lam = (jnp.exp(jnp.sum(lq1.astype(jnp.float32) * lk1.astype(jnp.float32)))
           - jnp.exp(jnp.sum(lq2.astype(jnp.float32) * lk2.astype(jnp.float32)))
           + lambda_init)
    o = diff_attention(q, k, v, lam)
    o = rms_norm(o, subln_g) * (1.0 - lambda_init)
    o = o.reshape(b, s, DA_V_WIDTH)

    g = spatial_gating(u, z, sg_ln_g, sg_ln_b, sg_w, sg_b)
    return jnp.concatenate([o, g], axis=-1) @ w_out


def odd_mixer(h, w_in, b_in, dw_w, dw_b, ln_g, ln_b, w_out, b_out):
    a = h @ w_in + b_in
    a, gate = jnp.split(a, 2, axis=-1)
    y = a * jax.nn.sigmoid(gate)
    y = causal_dwconv(y, dw_w, dw_b)
    y = jax.nn.silu(layer_norm(y, ln_g, ln_b))
    return y @ w_out + b_out


def conv_ffn(h, w_up, dw_w, dw_b, w_down):
    hh = causal_dwconv(h @ w_up, dw_w, dw_b)
    a, gate = jnp.split(hh, 2, axis=-1)
    return (a * jax.nn.silu(gate)) @ w_down


def setup_inputs(seed: int = 0) -> dict:
    key = jax.random.key(seed)
    ks = iter(jax.random.split(key, 40))
    f32 = jnp.float32

    def nrm(shape, scale):
        return jax.random.normal(next(ks), shape, f32) * scale

    def gain(shape):
        return 1.0 + nrm(shape, 0.02)

    d = D_MODEL
    return {
        "x": nrm((BATCH, SEQ, d), 1.0),
        "norm_mix_g": gain((DEPTH, d)),
        "norm_ffn_g": gain((DEPTH, d)),
        "ab_w_in": nrm((N_EVEN, d, AB_IN_WIDTH), d ** -0.5),
        "ab_w_out": nrm((N_EVEN, AB_OUT_WIDTH, d), AB_OUT_WIDTH ** -0.5),
        "diff_lq1": nrm((N_EVEN, DA_HEAD_DIM), 0.1),
        "diff_lk1": nrm((N_EVEN, DA_HEAD_DIM), 0.1),
        "diff_lq2": nrm((N_EVEN, DA_HEAD_DIM), 0.1),
        "diff_lk2": nrm((N_EVEN, DA_HEAD_DIM), 0.1),
        "diff_subln_g": gain((N_EVEN, DA_V_DIM)),
        "sg_ln_g": gain((N_EVEN, SG_WIDTH)),
        "sg_ln_b": nrm((N_EVEN, SG_WIDTH), 0.02),
        "sg_w": nrm((N_EVEN, SG_GROUPS, SG_CHUNK, SG_CHUNK), SG_CHUNK ** -0.5),
        "sg_b": gain((N_EVEN, SG_GROUPS, SG_CHUNK)),
        "conv_w_in": nrm((N_ODD, d, 2 * CONV_INNER), d ** -0.5),
        "conv_b_in": nrm((N_ODD, 2 * CONV_INNER), 0.02),
        "conv_dw_w": nrm((N_ODD, CONV_WIDTH, CONV_INNER), CONV_WIDTH ** -0.5),
        "conv_dw_b": nrm((N_ODD, CONV_INNER), 0.02),
        "conv_ln_g": gain((N_ODD, CONV_INNER)),
        "conv_ln_b": nrm((N_ODD, CONV_INNER), 0.02),
        "conv_w_out": nrm((N_ODD, CONV_INNER, d), CONV_INNER ** -0.5),
        "conv_b_out": nrm((N_ODD, d), 0.02),
        "ffn_w_up": nrm((DEPTH, d, 2 * FFN_HIDDEN), d ** -0.5),
        "ffn_dw_w": nrm((DEPTH, FFN_CONV_WIDTH, 2 * FFN_HIDDEN), FFN_CONV_WIDTH ** -0.5),
        "ffn_dw_b": nrm((DEPTH, 2 * FFN_HIDDEN), 0.02),
        "ffn_w_down": nrm((DEPTH, FFN_HIDDEN, d), FFN_HIDDEN ** -0.5),
        "final_norm_g": gain((d,)),
    }


def reference(x, norm_mix_g, norm_ffn_g, ab_w_in, ab_w_out, diff_lq1, diff_lk1,
              diff_lq2, diff_lk2, diff_subln_g, sg_ln_g, sg_ln_b, sg_w, sg_b,
              conv_w_in, conv_b_in, conv_dw_w, conv_dw_b, conv_ln_g, conv_ln_b,
              conv_w_out, conv_b_out, ffn_w_up, ffn_dw_w, ffn_dw_b, ffn_w_down,
              final_norm_g):
    seq = x.shape[1]
    cos, sin = rope_tables(seq, DA_HEAD_DIM)
    for l in range(DEPTH):
        h = rms_norm(x, norm_mix_g[l])
        i = l // 2
        if l % 2 == 0:
            x = x + even_mixer(h, l, ab_w_in[i], ab_w_out[i], diff_lq1[i],
                               diff_lk1[i], diff_lq2[i], diff_lk2[i],
                               diff_subln_g[i], sg_ln_g[i], sg_ln_b[i],
                               sg_w[i], sg_b[i], cos, sin)
        else:
            x = x + odd_mixer(h, conv_w_in[i], conv_b_in[i], conv_dw_w[i],
                              conv_dw_b[i], conv_ln_g[i], conv_ln_b[i],
                              conv_w_out[i], conv_b_out[i])
        x = x + conv_ffn(rms_norm(x, norm_ffn_g[l]), ffn_w_up[l], ffn_dw_w[l],
                         ffn_dw_b[l], ffn_w_down[l])
    return rms_norm(x, final_norm_g)
```

```python
import math
from contextlib import ExitStack

import numpy as np
import concourse.bass as bass
import concourse.mybir as mybir
from concourse.bass_utils import run_bass_kernel_spmd

F32 = mybir.dt.float32
BF16 = mybir.dt.bfloat16
I32 = mybir.dt.int32
ALU = mybir.AluOpType
AF = mybir.ActivationFunctionType

S = 2048
D = 1024
NT = 4
TT = 512
KC = 8
DEPTH = 4
EPS = 1e-6
FH = 2816
FC = 22
GRAN = 1024
COMPUTE = ("pe", "act", "dve", "pool")
N_DMA_SEMS = 28
N_SP_SEMS = 12
SPLIT_POOLS = True
FULL_SYNC = True
NEAR_SYNC = 4
SINF = AF.Sin
STAGE = 99
TD = 6


class Op:
    __slots__ = ("eng", "fn", "reads", "writes", "dma", "waits", "signal", "count",
                 "pos", "snap", "snap_dma", "dsem", "dval", "idx")


class Prog:
    def __init__(self):
        self.ops = []

    def add(self, eng, fn, reads=(), writes=(), dma=False):
        o = Op()
        o.eng = eng
        o.fn = fn
        o.reads = tuple(reads)
        o.writes = tuple(writes)
        o.dma = dma
        o.waits = []
        o.signal = False
        o.count = 0
        o.pos = -1
        o.idx = len(self.ops)
        self.ops.append(o)
        return o

    def analyse(self):
        ops = self.ops
        last_w = {}
        readers = {}
        engs = COMPUTE + ("sp",)
        known = {e: {f: -1 for f in COMPUTE} for e in engs}
        known_dma = {e: frozenset() for e in engs}
        eng_pos = {e: 0 for e in COMPUTE}
        dma_idx = {"sp": 0, "pool": 0}
        dma_hist = {}
        for i, op in enumerate(ops):
            E = op.eng
            deps = set()
            for r in op.reads:
                w = last_w.get(r)
                if w is not None:
                    deps.add(w)
                if r[0] == "PS":
                    for rd in readers.get(r, ()):
                        if ops[rd].eng != E:
                            deps.add(rd)
            for r in op.writes:
                w = last_w.get(r)
                if w is not None:
                    deps.add(w)
                for rd in readers.get(r, ()):
                    deps.add(rd)
            if op.dma:
                if SPLIT_POOLS:
                    base, npool = (0, N_SP_SEMS) if E == "sp" else (N_SP_SEMS, N_DMA_SEMS - N_SP_SEMS)
                    di = dma_idx[E]
                else:
                    base, npool = 0, 24
                    di = dma_idx["sp"] + dma_idx["pool"]
                op.dsem = base + di % npool
                op.dval = 16 * (di // npool + 1)
                prev = dma_hist.get(op.dsem)
                if prev is not None:
                    deps.add(prev)
                dma_hist[op.dsem] = i
                dma_idx[E] += 1
            deps.discard(i)
            kn = known[E]
            kd = known_dma[E]
            wset = None
            for j in sorted(deps):
                oj = ops[j]
                if oj.dma:
                    if j in kd:
                        continue
                    op.waits.append(j)
                    kd = kd | oj.snap_dma | frozenset((j,))
                    for f, v in oj.snap.items():
                        if v > kn[f]:
                            kn[f] = v
                else:
                    F = oj.eng
                    if F == E and E == "pe":
                        continue
                    if F == E and not FULL_SYNC:
                        if wset is None:
                            wset = set(op.reads)
                        if not any(r in wset for r in oj.writes) and eng_pos[E] - oj.pos > NEAR_SYNC:
                            continue
                    if kn[F] >= oj.pos:
                        continue
                    op.waits.append(j)
                    kn[F] = oj.pos
                    for f, v in oj.snap.items():
                        if v > kn[f]:
                            kn[f] = v
                    kd = kd | oj.snap_dma
            known_dma[E] = kd
            op.snap = dict(kn)
            op.snap_dma = kd
            for r in op.reads:
                readers.setdefault(r, []).append(i)
            for r in op.writes:
                last_w[r] = i
                readers[r] = []
            if not op.dma and E in eng_pos:
                op.pos = eng_pos[E]
                eng_pos[E] += 1
        for op in ops:
            for j in op.waits:
                if not ops[j].dma:
                    ops[j].signal = True
        cnt = {e: 0 for e in COMPUTE}
        for op in ops:
            if op.signal:
                cnt[op.eng] += 1
                op.count = cnt[op.eng]

    def emit(self, nc, es):
        self.analyse()
        ops = self.ops
        esem = {e: es.enter_context(nc.semaphore("s_" + e)) for e in COMPUTE}
        dsem = [es.enter_context(nc.semaphore("d_%d" % i)) for i in range(N_DMA_SEMS)]
        per_eng = {e: [] for e in COMPUTE + ("sp",)}
        for op in ops:
            per_eng[op.eng].append(op)

        def run(name, e):
            for op in per_eng[name]:
                for j in op.waits:
                    oj = ops[j]
                    if oj.dma:
                        e.wait_ge(dsem[oj.dsem], oj.dval)
                    else:
                        e.wait_ge(esem[oj.eng], oj.count)
                if op.fn is None:
                    continue
                ins = op.fn(e)
                if op.dma:
                    ins.then_inc(dsem[op.dsem], 16)
                elif op.signal:
                    ins.then_inc(esem[op.eng], 1)

        with nc.Block() as block:
            @block.tensor
            def _(e):
                run("pe", e)

            @block.scalar
            def _(e):
                run("act", e)

            @block.vector
            def _(e):
                run("dve", e)

            @block.gpsimd
            def _(e):
                run("pool", e)

            @block.sync
            def _(e):
                run("sp", e)


class Buf:
    def __init__(self, nc, es, name, n, dtype, parts=128):
        self.name = name
        self.n = n
        self.dtype = dtype
        self.esz = 4 if dtype in (F32, I32) else 2
        self.t = es.enter_context(nc.sbuf_tensor(name, [parts, n], dtype))

    def keys(self, off, n, dt=None):
        sz = self.esz if dt is None else (4 if dt in (F32, I32) else 2)
        b0 = off * sz
        b1 = (off + n) * sz - 1
        return [(self.name, g) for g in range(b0 // GRAN, b1 // GRAN + 1)]

    def v(self, off, n, p0=0, p1=128, dt=None):
        if dt is None or dt == self.dtype:
            return self.t[p0:p1, off:off + n]
        sz = 4 if dt in (F32, I32) else 2
        b0 = off * sz // self.esz
        bn = n * sz // self.esz
        return self.t[p0:p1, b0:b0 + bn].bitcast(dt)

    def v3(self, off, k, n, p0=0, p1=128, dt=None):
        return self.v(off, k * n, p0, p1, dt).rearrange("p (k n) -> p k n", k=k)


def build_program(layers=(0, 1, 2, 3), final_norm=True, do_mixer=True, do_ffn=True):
    nc = bass.Bass("TRN2", target_bir_lowering=False)
    dr = {}

    def dram(name, shape, kind="ExternalInput"):
        dr[name] = nc.dram_tensor(name, list(shape), F32, kind=kind).ap()
        return dr[name]

    x_d = dram("x", [S, D])
    dram("norm_mix_g", [4, D])
    dram("norm_ffn_g", [4, D])
    dram("ab_w_in", [2, D, 2560])
    dram("ab_w_out", [2, D, D])
    for nm in ("diff_lq1", "diff_lk1", "diff_lq2", "diff_lk2"):
        dram(nm, [2, 64])
    dram("diff_subln_g", [2, 128])
    dram("sg_ln_g", [2, 512])
    dram("sg_ln_b", [2, 512])
    dram("sg_w", [2, 4, 128, 128])
    dram("sg_b", [2, 4, 128])
    dram("conv_w_in", [2, D, 2 * D])
    dram("conv_b_in", [2, 2 * D])
    dram("conv_dw_w", [2, 31, D])
    dram("conv_dw_b", [2, D])
    dram("conv_ln_g", [2, D])
    dram("conv_ln_b", [2, D])
    dram("conv_w_out", [2, D, D])
    dram("conv_b_out", [2, D])
    dram("ffn_w_up", [4, D, 2 * FH])
    dram("ffn_dw_w", [4, 3, 2 * FH])
    dram("ffn_dw_b", [4, 2 * FH])
    dram("ffn_w_down", [4, FH, D])
    dram("final_norm_g", [D])
    y_d = dram("y", [S, D], kind="ExternalOutput")
    tab_d = nc.dram_tensor("rope_tab", [128, 4096], F32, kind="Internal").ap()
    tab_state = [False]
    tab_gen = [False]

    P = Prog()
    es = ExitStack()
    with es:
        X = Buf(nc, es, "X", KC * S, F32)
        H = Buf(nc, es, "H", KC * S, BF16)
        A = Buf(nc, es, "A", KC * S, BF16)
        B = Buf(nc, es, "B", KC * S, BF16)
        C = Buf(nc, es, "C", 8192, BF16)
        W = Buf(nc, es, "W", 6144, BF16)
        T = Buf(nc, es, "T", 5 * TT, F32)
        Q = Buf(nc, es, "Q", 4 * TT, BF16)
        IDB = Buf(nc, es, "IDB", 128, BF16)
        RM = Buf(nc, es, "RM", 128, BF16)
        OV = Buf(nc, es, "OV", 48, F32)
        DWT = Buf(nc, es, "DWT", 248, F32)
        SM = Buf(nc, es, "SM", 64, F32)
        MASK = Buf(nc, es, "MASK", 128, BF16)
        IOTC = Buf(nc, es, "IOTC", 128, F32)
        CM = Buf(nc, es, "CM", 512, BF16)
        IDF = Buf(nc, es, "IDF", 128, F32)
        ONES = Buf(nc, es, "ONES", 128, BF16)
        GV = Buf(nc, es, "GV", 9 * 8, F32)
        FD = Buf(nc, es, "FD", 4 * 44, F32)
        PS = es.enter_context(nc.psum_tensor("PS", [128, 8, TT], F32))

        class _View:
            def __init__(self, buf, base, dt):
                self.buf, self.base, self.dt = buf, base, dt

            def v(self, off, n, p0=0, p1=128):
                return self.buf.v(self.base + off, n, p0, p1, dt=self.dt)

            def keys(self, off, n):
                return self.buf.keys(self.base + off, n, dt=self.dt)

        VS = _View(Q, 512, F32)
        VS.v3 = lambda off, k, n, p0=0, p1=128: Q.v3(512 + off, k, n, p0, p1, dt=F32)
        IOT = _View(T, 0, F32)
        TI = _View(T, 0, I32)
        SMI = _View(SM, 32, I32)
        bank_ctr = [0]

        reserved_banks = set()

        def bank():
            while True:
                b = bank_ctr[0] % 8
                bank_ctr[0] += 1
                if b not in reserved_banks:
                    return b

        def psk(b):
            return [("PS", b)]

        def ps(b, n=TT, off=0, p0=0, p1=128):
            return PS[p0:p1, b, off:off + n]

        def consts():
            P.add("pool", lambda e: e.iota(IOT.v(0, 128), [[1, 128]], base=0, channel_multiplier=-1,
                                           allow_small_or_imprecise_dtypes=True),
                  writes=IOT.keys(0, 128))
            P.add("pool", lambda e: e.iota(IOTC.v(0, 128), [[1, 128]], base=0, channel_multiplier=0,
                                           allow_small_or_imprecise_dtypes=True),
                  writes=IOTC.keys(0, 128))
            P.add("pool", lambda e: e.iota(SM.v(3, 1), [[0, 1]], base=0, channel_multiplier=1,
                                           allow_small_or_imprecise_dtypes=True),
                  writes=SM.keys(3, 1))
            P.add("pool", lambda e: e.iota(SM.v(16, 16), [[128, 16]], base=0, channel_multiplier=0,
                                           allow_small_or_imprecise_dtypes=True),
                  writes=SM.keys(16, 16))
            P.add("dve", lambda e: e.tensor_single_scalar(IDF.v(0, 128), IOT.v(0, 128), 0.0, ALU.is_equal),
                  reads=IOT.keys(0, 128) + IOTC.keys(0, 128) + SM.keys(0, 64), writes=IDF.keys(0, 128))
            P.add("dve", lambda e: e.memset(ONES.v(0, 128), 1.0), writes=ONES.keys(0, 128))
            P.add("dve", lambda e: e.tensor_single_scalar(IDB.v(0, 128), IOT.v(0, 128), 0.0, ALU.is_equal),
                  reads=IOT.keys(0, 128), writes=IDB.keys(0, 128))
            P.add("dve", lambda e: e.tensor_single_scalar(MASK.v(0, 128), IOT.v(0, 128), 0.0, ALU.is_ge),
                  reads=IOT.keys(0, 128), writes=MASK.keys(0, 128))
            NEG = -30000.0
            for (off, op, thr_lo, thr_hi, mul) in ((0, ALU.is_equal, 0.0, -64.0, 1.0), (128, ALU.is_equal, 64.0, 0.0, 1.0),
                                                   (256, ALU.is_lt, 0.0, -64.0, NEG), (384, ALU.is_lt, 64.0, 0.0, NEG)):
                for (p0, thr) in ((0, thr_lo), (64, thr_hi)):
                    P.add("dve", lambda e, off=off, op=op, p0=p0, thr=thr, mul=mul: e.tensor_scalar(
                        CM.v(off, 128, p0, p0 + 64), IOT.v(0, 128, p0, p0 + 64), thr, mul, op, ALU.mult),
                        reads=IOT.keys(0, 128), writes=CM.keys(0, 512))
            for (d0, s0, sgn) in ((0, 32, -1.0), (32, 0, 1.0), (64, 96, -1.0), (96, 64, 1.0)):
                P.add("dve", lambda e, d0=d0, s0=s0, sgn=sgn: e.tensor_scalar(
                    RM.v(d0, 32), IDB.v(s0, 32), sgn, None, ALU.mult),
                    reads=IDB.keys(0, 128), writes=RM.keys(0, 128))

        def load_rows_T(dst_buf, dst_off, rows, Cn):
            R = len(rows)
            assert R * 128 <= 512 and R * Cn <= TT
            for r, row in enumerate(rows):
                P.add("sp", lambda e, r=r, row=row: e.dma_start(
                    out=VS.v(r * 128, 128, 0, Cn), in_=row.rearrange("(c p) -> c p", p=128)),
                    writes=VS.keys(r * 128, 128), dma=True)
            pb = bank()
            for r in range(R):
                P.add("pe", lambda e, r=r: e.transpose(ps(pb, Cn, r * Cn), VS.v(r * 128, 128, 0, Cn),
                                                       IDF.v(0, Cn, 0, Cn)),
                      reads=VS.keys(r * 128, 128) + IDF.keys(0, 128), writes=psk(pb))
            P.add("dve", lambda e: e.tensor_copy(dst_buf.v(dst_off, R * Cn), ps(pb, R * Cn)),
                  reads=psk(pb), writes=dst_buf.keys(dst_off, R * Cn))

        def load_x():
            for blk in range(16):
                sbuf_ = A if blk < 8 else B
                st = (blk % 8) * 1024
                P.add("sp", lambda e, blk=blk, st=st, sbuf_=sbuf_: e.dma_start(
                    out=sbuf_.v(st, 1024, dt=F32), in_=x_d[blk * 128:(blk + 1) * 128, :]),
                    writes=sbuf_.keys(st, 1024, dt=F32), dma=True)
            for blk in range(16):
                sbuf_ = A if blk < 8 else B
                st = (blk % 8) * 1024
                for c4 in range(2):
                    pb = bank()
                    for cc in range(4):
                        c = c4 * 4 + cc
                        P.add("pe", lambda e, pb=pb, cc=cc, c=c, st=st, sbuf_=sbuf_: e.transpose(
                            ps(pb, 128, cc * 128), sbuf_.v(st + c * 128, 128, dt=F32), IDF.v(0, 128)),
                            reads=sbuf_.keys(st + c * 128, 128, dt=F32) + IDF.keys(0, 128), writes=psk(pb))
                    keys = []
                    for cc in range(4):
                        keys += X.keys((c4 * 4 + cc) * S + blk * 128, 128)
                    dst = X.t[:, :].rearrange("p (c t) -> p c t", c=KC)[:, c4 * 4:(c4 + 1) * 4, blk * 128:(blk + 1) * 128]
                    src = PS[:, pb, :].rearrange("p (c t) -> p c t", c=4)
                    P.add("act", lambda e, dst=dst, src=src: e.copy(dst, src), reads=psk(pb), writes=keys)

        def store_y(src_buf):
            for blk in range(16):
                st = (blk % 2) * 1024
                for c4 in range(2):
                    pb = bank()
                    for cc in range(4):
                        c = c4 * 4 + cc
                        P.add("pe", lambda e, pb=pb, cc=cc, c=c, blk=blk: e.transpose(
                            ps(pb, 128, cc * 128), src_buf.v(c * S + blk * 128, 128), IDF.v(0, 128)),
                            reads=src_buf.keys(c * S + blk * 128, 128) + IDF.keys(0, 128), writes=psk(pb))
                    if c4 == 0:
                        P.add("act", lambda e, pb=pb, st=st: e.copy(T.v(st, 512), ps(pb)),
                              reads=psk(pb), writes=T.keys(st, 512))
                    else:
                        P.add("dve", lambda e, pb=pb, st=st: e.tensor_copy(T.v(st + 512, 512), ps(pb)),
                              reads=psk(pb), writes=T.keys(st + 512, 512))
                P.add("sp", lambda e, blk=blk, st=st: e.dma_start(
                    out=y_d[blk * 128:(blk + 1) * 128, :], in_=T.v(st, 1024)),
                    reads=T.keys(st, 1024), writes=[("Y", blk)], dma=True)
            P.add("sp", None, reads=[("Y", b) for b in range(16)])

        def rms_stats(t, src_buf, nchunks, stride, inv_n):
            pb = bank()
            for c in range(nchunks):
                q = (c % 2) * TT
                P.add("act", lambda e, c=c, q=q: e.activation(Q.v(q, TT), src_buf.v(c * stride + t * TT, TT), AF.Square),
                      reads=src_buf.keys(c * stride + t * TT, TT), writes=Q.keys(q, TT))
                P.add("pe", lambda e, c=c, q=q: e.matmul(ps(pb), ONES.v(0, 128), Q.v(q, TT),
                                                         start=(c == 0), stop=(c == nchunks - 1)),
                      reads=Q.keys(q, TT) + ONES.keys(0, 128), writes=psk(pb))
            o1 = 4 * TT
            o2 = 4 * TT
            P.add("act", lambda e: e.activation(T.v(o1, TT), ps(pb), AF.Ln, bias=EPSB.v(0, 1), scale=inv_n),
                  reads=psk(pb) + EPSB.keys(0, 1), writes=T.keys(o1, TT))
            P.add("act", lambda e: e.activation(T.v(o2, TT), T.v(o1, TT), AF.Exp, scale=-0.5),
                  reads=T.keys(o1, TT), writes=T.keys(o2, TT))
            return o2

        norm_state = {"grow": None, "inplace": False, "done": set()}

        def norm_tile(grow, t, inplace):
            dstb = X if inplace else H
            o2 = rms_stats(t, X, KC, S, 1.0 / D)
            for c in range(KC):
                P.add("dve", lambda e, c=c, t=t: e.scalar_tensor_tensor(
                    dstb.v(c * S + t * TT, TT), X.v(c * S + t * TT, TT), GV.v(grow * 8 + c, 1),
                    T.v(o2, TT), ALU.mult, ALU.mult),
                    reads=X.keys(c * S + t * TT, TT) + GV.keys(grow * 8 + c, 1) + T.keys(o2, TT),
                    writes=dstb.keys(c * S + t * TT, TT))

        def norm_prepare(grow, inplace=False):
            norm_state["grow"] = grow
            norm_state["inplace"] = inplace
            norm_state["done"] = set()

        def norm_epilogue(t):
            if norm_state["grow"] is not None and t not in norm_state["done"]:
                norm_state["done"].add(t)
                norm_tile(norm_state["grow"], t, norm_state["inplace"])

        def rmsnorm_to_H(grow):
            if norm_state["grow"] != grow or norm_state["inplace"]:
                norm_prepare(grow, False)
            for t in range(NT):
                norm_epilogue(t)
            norm_state["grow"] = None

        def final_rmsnorm_inplace(grow):
            if norm_state["grow"] != grow or not norm_state["inplace"]:
                norm_prepare(grow, True)
            for t in range(NT):
                norm_epilogue(t)
            norm_state["grow"] = None

        EPSB = Buf(nc, es, "EPSB", 8, F32)

        def wload(dst_ap, dst_keys, src_ap):
            P.add("pool", lambda e: e.dma_start(out=dst_ap, in_=src_ap), writes=dst_keys, dma=True)

        hook = [None]
        stepq = []
        step_issued = [False]

        def tick():
            if not stepq:
                return
            if not step_issued[0]:
                stepq[0][0]()
                step_issued[0] = True
            else:
                stepq[0][1]()
                stepq.pop(0)
                step_issued[0] = False
                if stepq:
                    stepq[0][0]()
                    step_issued[0] = True

        def drain():
            while stepq:
                tick()

        def run_hook():
            if hook[0] is not None:
                f = hook[0]
                hook[0] = None
                f()
                tick()

        def rows_T_step(dst_buf, dst_off, rows, Cn):
            R = len(rows)
            assert R * 128 <= 512 and R * Cn <= TT

            def issue():
                for r, row in enumerate(rows):
                    P.add("sp", lambda e, r=r, row=row: e.dma_start(
                        out=VS.v(r * 128, 128, 0, Cn), in_=row.rearrange("(c p) -> c p", p=128)),
                        writes=VS.keys(r * 128, 128), dma=True)

            def consume():
                pb = bank()
                for r in range(R):
                    P.add("pe", lambda e, r=r: e.transpose(ps(pb, Cn, r * Cn), VS.v(r * 128, 128, 0, Cn),
                                                           IDF.v(0, Cn, 0, Cn)),
                          reads=VS.keys(r * 128, 128) + IDF.keys(0, 128), writes=psk(pb))
                P.add("dve", lambda e: e.tensor_copy(dst_buf.v(dst_off, R * Cn), ps(pb, R * Cn)),
                      reads=psk(pb), writes=dst_buf.keys(dst_off, R * Cn))
            stepq.append((issue, consume))

        def pre_ffn(l):
            rows = [dr["ffn_dw_w"][l, 0], dr["ffn_dw_w"][l, 1], dr["ffn_dw_w"][l, 2], dr["ffn_dw_b"][l]]
            rows_T_step(FD, 0, rows, 44)

        def ffn(l):
            w_up = dr["ffn_w_up"][l]
            w_dn = dr["ffn_w_down"][l]
            rmsnorm_to_H(4 + l)
            if ffn_norm_announce[0] is not None:
                norm_prepare(*ffn_norm_announce[0])
            run_hook()
            UB = 1536
            WN = 1024
            groups = [(0, 7), (7, 14), (14, 22)]
            it = [0]
            pending = []
            down_q = []

            def emit_down(gi_, c0_, c1_, wd_off_, j0=0, j1=None, epilogue=True):
                G_ = c1_ - c0_
                if j1 is None:
                    j1 = G_
                bank_ctr[0] = (it[0] % 2) * 4
                wd = A.v3(wd_off_, G_, D)
                for t in range(NT):
                    for n in range(KC):
                        pb = bank()
                        for j_ in range(j0, j1):
                            sl_ = (c0_ + j_) % 8
                            P.add("pe", lambda e, pb=pb, j_=j_, sl_=sl_, n=n, t=t, wd=wd: e.matmul(
                                ps(pb), wd[:, j_, n * 128:(n + 1) * 128], B.v(sl_ * S + t * TT, TT),
                                start=(j_ == j0), stop=(j_ == j1 - 1)),
                                reads=A.keys(wd_off_ + j_ * D, D) + B.keys(sl_ * S + t * TT, TT), writes=psk(pb))
                        P.add("dve", lambda e, pb=pb, n=n, t=t: e.tensor_tensor(
                            X.v(n * S + t * TT, TT), ps(pb), X.v(n * S + t * TT, TT), ALU.add),
                            reads=psk(pb) + X.keys(n * S + t * TT, TT), writes=X.keys(n * S + t * TT, TT))
                    if gi_ == len(groups) - 1 and epilogue and t >= 1:
                        norm_epilogue(t - 1)

            def flush_pending():
                while pending:
                    (i, j, tp) = pending.pop(0)
                    ygo = 4 * UB + i * WN
                    yao = i * WN
                    P.add("act", lambda e, ygo=ygo: e.activation(C.v(ygo, WN), C.v(ygo, WN), AF.Silu),
                          reads=C.keys(ygo, WN), writes=C.keys(ygo, WN))
                    P.add("dve", lambda e, ygo=ygo, yao=yao, j=j, tp=tp: e.tensor_tensor(
                        B.v(j * S + tp * WN, WN), T.v(yao, WN), C.v(ygo, WN), ALU.mult),
                        reads=T.keys(yao, WN) + C.keys(ygo, WN), writes=B.keys(j * S + tp * WN, WN))

            for gi, (c0, c1) in enumerate(groups):
                G = c1 - c0
                wd_off = (gi % 2) * 8192
                wload(A.v3(wd_off, G, D), A.keys(wd_off, G * D),
                      w_dn[c0 * 128:c1 * 128, :].rearrange("(j p) n -> p j n", p=128))
                for c in range(c0, c1):
                    j = c % 8
                    so = (c % 3) * 2048
                    slot = W.v3(so, KC, 256)
                    wload(slot[:, :, 0:128], W.keys(so, 2048),
                          w_up[:, c * 128:(c + 1) * 128].rearrange("(k p) n -> p k n", p=128))
                    wload(slot[:, :, 128:256], W.keys(so, 2048),
                          w_up[:, FH + c * 128:FH + (c + 1) * 128].rearrange("(k p) n -> p k n", p=128))
                    if c == c0 + 1 and down_q:
                        emit_down(*down_q.pop(0))
                    for tp in range(2):
                        i = it[0] % 2
                        it[0] += 1
                        base = i * 4
                        uoffs = ((2 * i) * UB, (2 * i + 1) * UB)
                        unext = ((2 * (1 - i)) * UB, (2 * (1 - i) + 1) * UB)
                        yao = i * WN
                        ygo = 4 * UB + i * WN
                        for (bo, co) in ((0, 0), (2, 128)):
                            for half in range(2):
                                t = tp * 2 + half
                                pb = base + bo + half
                                for k in range(KC):
                                    P.add("pe", lambda e, pb=pb, co=co, k=k, t=t, slot=slot: e.matmul(
                                        ps(pb), slot[:, k, co:co + 128], H.v(k * S + t * TT, TT),
                                        start=(k == 0), stop=(k == KC - 1)),
                                        reads=W.keys(so, 2048) + H.keys(k * S + t * TT, TT), writes=psk(pb))
                        for pi, (bo, f) in enumerate(((0, c), (2, FC + c))):
                            uo = uoffs[pi]
                            src = PS[:, base + bo:base + bo + 2, :]
                            pk = psk(base + bo) + psk(base + bo + 1)
                            P.add("act", lambda e, uo=uo, src=src: e.copy(C.v3(uo + 2, 2, TT), src),
                                  reads=pk, writes=C.keys(uo + 2, WN))
                            if pi == 0:
                                P.add("act", lambda e, src=src, f=f, yao=yao: e.activation(
                                    T.v3(yao, 2, TT), src, AF.Identity, bias=FD.v(3 * 44 + f, 1), scale=FD.v(2 * 44 + f, 1)),
                                    reads=pk + FD.keys(0, 176), writes=T.keys(yao, WN))
                            else:
                                P.add("act", lambda e, src=src, f=f, ygo=ygo: e.activation(
                                    C.v3(ygo, 2, TT), src, AF.Identity, bias=FD.v(3 * 44 + f, 1), scale=FD.v(2 * 44 + f, 1)),
                                    reads=pk + FD.keys(0, 176), writes=C.keys(ygo, WN))
                            if tp == 0:
                                un = unext[pi]
                                P.add("act", lambda e, uo=uo, un=un: e.copy(C.v(un, 2), C.v(uo + WN, 2)),
                                      reads=C.keys(uo + WN, 2), writes=C.keys(un, 2))
                        for pi, f in enumerate((c, FC + c)):
                            uo = uoffs[pi]
                            for (tap, sh) in ((1, 1), (0, 2)):
                                lo = sh if tp == 0 else 0
                                nn = WN - lo
                                if pi == 0:
                                    yv = T.v(yao + lo, nn)
                                    yk = T.keys(yao, WN)
                                else:
                                    yv = C.v(ygo + lo, nn)
                                    yk = C.keys(ygo, WN)
                                P.add("dve", lambda e, uo=uo, yv=yv, f=f, tap=tap, sh=sh, lo=lo, nn=nn: e.scalar_tensor_tensor(
                                    yv, C.v(uo + 2 + lo - sh, nn), FD.v(tap * 44 + f, 1), yv, ALU.mult, ALU.add),
                                    reads=C.keys(uo + 2 + lo - sh, nn) + FD.keys(0, 176) + yk, writes=yk)
                        flush_pending()
                        pending.append((i, j, tp))
                    tick()
                down_q.append((gi, c0, c1, wd_off))
            flush_pending()
            while down_q:
                g_ = down_q.pop(0)
                if g_[0] == len(groups) - 1:
                    Gl = g_[2] - g_[1]
                    emit_down(*g_, j0=0, j1=Gl - 1, epilogue=False)
                    emit_down(*g_, j0=Gl - 1, j1=Gl, epilogue=True)
                else:
                    emit_down(*g_)

        def proj_fm(slot, so, ncols_off, t, pb):
            for k in range(KC):
                P.add("pe", lambda e, k=k: e.matmul(ps(pb), slot[:, k, ncols_off:ncols_off + 128],
                                                    H.v(k * S + t * TT, TT), start=(k == 0), stop=(k == KC - 1)),
                      reads=W.keys(so, 2048) + H.keys(k * S + t * TT, TT), writes=psk(pb))

        def out_proj(w_out, kin, wbuf, woff, bias_col=None, tiles=(0, 1, 2, 3), load=True):
            wv = wbuf.v3(woff, KC, D)
            if load:
                for hf in range(2):
                    wload(wv[:, hf * 4:(hf + 1) * 4, :], wbuf.keys(woff + hf * 4096, 4096),
                          w_out[hf * 512:(hf + 1) * 512, :].rearrange("(k p) n -> p k n", p=128))
            for t in tiles:
                for n in range(KC):
                    pb = bank()
                    for k in range(KC):
                        kb, ko = kin[k]
                        P.add("pe", lambda e, k=k, kb=kb, ko=ko, t=t, pb=pb, n=n: e.matmul(
                            ps(pb), wv[:, k, n * 128:(n + 1) * 128], kb.v(ko + t * TT, TT), start=(k == 0), stop=(k == KC - 1)),
                            reads=wbuf.keys(woff + k * D, D) + kb.keys(ko + t * TT, TT), writes=psk(pb))
                    xk = X.keys(n * S + t * TT, TT)
                    if bias_col is None:
                        P.add("dve", lambda e, pb=pb, n=n, t=t: e.tensor_tensor(
                            X.v(n * S + t * TT, TT), ps(pb), X.v(n * S + t * TT, TT), ALU.add),
                            reads=psk(pb) + xk, writes=xk)
                    else:
                        P.add("dve", lambda e, pb=pb, n=n, t=t: e.scalar_tensor_tensor(
                            X.v(n * S + t * TT, TT), ps(pb), OV.v(bias_col + n, 1), X.v(n * S + t * TT, TT),
                            ALU.add, ALU.add),
                            reads=psk(pb) + xk + OV.keys(0, 48), writes=xk)
                if t >= 1:
                    norm_epilogue(t - 1)

        def pre_odd(l):
            i = l // 2
            rows_T_step(OV, 0, [dr["conv_b_in"][i, 0:D], dr["conv_b_in"][i, D:2 * D], dr["conv_dw_b"][i],
                                dr["conv_ln_g"][i]], 8)
            rows_T_step(OV, 32, [dr["conv_ln_b"][i], dr["conv_b_out"][i]], 8)
            for half in range(2):
                def issue(half=half):
                    P.add("sp", lambda e: e.dma_start(
                        out=VS.v(0, 512, 0, 31), in_=dr["conv_dw_w"][i, :, half * 512:(half + 1) * 512]),
                        writes=VS.keys(0, 512), dma=True)

                def consume(half=half):
                    pb = bank()
                    for cc in range(4):
                        P.add("pe", lambda e, cc=cc, pb=pb: e.transpose(ps(pb, 31, cc * 31), VS.v(cc * 128, 128, 0, 31),
                                                                        IDF.v(0, 31, 0, 31)),
                              reads=VS.keys(0, 512) + IDF.keys(0, 128), writes=psk(pb))
                    P.add("dve", lambda e, pb=pb: e.tensor_copy(DWT.v(half * 124, 124), ps(pb, 124)),
                          reads=psk(pb), writes=DWT.keys(0, 248))
                stepq.append((issue, consume))

        def odd_mixer(l):
            i = l // 2
            w_in = dr["conv_w_in"][i]
            w_out = dr["conv_w_out"][i]
            rmsnorm_to_H(l)
            if mixer_norm_announce[0] is not None:
                norm_prepare(mixer_norm_announce[0], False)
            run_hook()
            def build_diag(c):
                dgo = (c % 2) * 4096
                dg = C.v3(dgo, 31, 128)
                for k in range(31):
                    P.add("dve", lambda e, k=k, c=c, dg=dg: e.tensor_scalar(
                        dg[:, k, :], IDB.v(0, 128), DWT.v(c * 31 + k, 1), None, ALU.mult),
                        reads=IDB.keys(0, 128) + DWT.keys(0, 248), writes=C.keys(dgo + k * 128, 128))
            build_diag(0)
            build_diag(1)
            for c in range(KC):
                so = (c % 3) * 2048
                slot = W.v3(so, KC, 256)
                wload(slot[:, :, 0:128], W.keys(so, 2048),
                      w_in[:, c * 128:(c + 1) * 128].rearrange("(k p) n -> p k n", p=128))
                wload(slot[:, :, 128:256], W.keys(so, 2048),
                      w_in[:, D + c * 128:D + (c + 1) * 128].rearrange("(k p) n -> p k n", p=128))
                for t in range(NT):
                    pa = bank()
                    pg = bank()
                    proj_fm(slot, so, 0, t, pa)
                    proj_fm(slot, so, 128, t, pg)
                    sg = (t % 2) * TT
                    P.add("act", lambda e, pg=pg, sg=sg, c=c: e.activation(
                        T.v(sg, TT), ps(pg), AF.Sigmoid, bias=OV.v(8 + c, 1)),
                        reads=psk(pg) + OV.keys(0, 48), writes=T.keys(sg, TT))
                    P.add("dve", lambda e, pa=pa, sg=sg, c=c, t=t: e.scalar_tensor_tensor(
                        A.v(c * S + t * TT, TT), ps(pa), OV.v(c, 1), T.v(sg, TT), ALU.add, ALU.mult),
                        reads=psk(pa) + OV.keys(0, 48) + T.keys(sg, TT), writes=A.keys(c * S + t * TT, TT))
                tick()
            for c in range(KC):
                dgo = (c % 2) * 4096
                dg = C.v3(dgo, 31, 128)
                if c >= 2:
                    build_diag(c)
                for t in range(NT):
                    pb = bank()
                    for kk in range(31 - TD):
                        k = 30 - kk
                        sh = 30 - k
                        lo = sh if t == 0 else 0
                        nn = TT - lo
                        P.add("pe", lambda e, k=k, sh=sh, lo=lo, nn=nn, t=t, c=c, pb=pb, dg=dg: e.matmul(
                            ps(pb, nn, lo), dg[:, k, :], A.v(c * S + t * TT + lo - sh, nn),
                            start=(k == 30), stop=(k == TD)),
                            reads=C.keys(dgo + k * 128, 128) + A.keys(c * S + t * TT + lo - sh, nn), writes=psk(pb))
                    ac = ((c * NT + t) % 2) * TT
                    ak = T.keys(ac, TT)
                    first = True
                    if t == 0:
                        P.add("dve", lambda e, ac=ac: e.memset(T.v(ac, TT), 0.0), writes=ak)
                        first = False
                    for k in range(TD):
                        sh = 30 - k
                        lo = sh if t == 0 else 0
                        nn = TT - lo
                        src = A.v(c * S + t * TT + lo - sh, nn)
                        sk = A.keys(c * S + t * TT + lo - sh, nn)
                        if first:
                            P.add("dve", lambda e, ac=ac, src=src, k=k, c=c: e.tensor_scalar(
                                T.v(ac, TT), src, DWT.v(c * 31 + k, 1), None, ALU.mult),
                                reads=sk + DWT.keys(0, 248), writes=ak)
                            first = False
                        else:
                            P.add("dve", lambda e, ac=ac, src=src, k=k, c=c, lo=lo, nn=nn: e.scalar_tensor_tensor(
                                T.v(ac + lo, nn), src, DWT.v(c * 31 + k, 1), T.v(ac + lo, nn), ALU.mult, ALU.add),
                                reads=sk + DWT.keys(0, 248) + ak, writes=ak)
                    P.add("dve", lambda e, ac=ac, pb=pb: e.tensor_tensor(T.v(ac, TT), ps(pb), T.v(ac, TT), ALU.add),
                          reads=psk(pb) + ak, writes=ak)
                    P.add("act", lambda e, ac=ac, c=c, t=t: e.activation(
                        B.v(c * S + t * TT, TT), T.v(ac, TT), AF.Identity, bias=OV.v(16 + c, 1)),
                        reads=ak + OV.keys(0, 48), writes=B.keys(c * S + t * TT, TT))
                    P.add("act", lambda e, ac=ac, c=c, t=t: e.activation(
                        H.v(c * S + t * TT, TT), T.v(ac, TT), AF.Square, bias=OV.v(16 + c, 1)),
                        reads=ak + OV.keys(0, 48), writes=H.keys(c * S + t * TT, TT))
                tick()
            out_proj(w_out, [(A, k * S) for k in range(KC)], C, 0, bias_col=40, tiles=(), load=True)
            ln_banks = {}

            reserved_banks.update((0, 1, 2, 3))

            def ln_stats(t):
                p1 = (t % 2) * 2
                p2 = p1 + 1
                for (pb, src) in ((p1, B), (p2, H)):
                    for c in range(KC):
                        P.add("pe", lambda e, pb=pb, src=src, c=c, t=t: e.matmul(
                            ps(pb), ONES.v(0, 128), src.v(c * S + t * TT, TT), start=(c == 0), stop=(c == KC - 1)),
                            reads=ONES.keys(0, 128) + src.keys(c * S + t * TT, TT), writes=psk(pb))
                ln_banks[t] = (p1, p2)

            def ln_ew(t):
                p1, p2 = ln_banks[t]
                m_, v_, nb_, u_ = 0, TT, 2 * TT, 3 * TT
                P.add("act", lambda e, p1=p1: e.activation(T.v(m_, TT), ps(p1), AF.Copy, scale=1.0 / D),
                      reads=psk(p1), writes=T.keys(m_, TT))
                P.add("act", lambda e, p1=p1: e.activation(T.v(v_, TT), ps(p1), AF.Square, scale=1.0 / D),
                      reads=psk(p1), writes=T.keys(v_, TT))
                P.add("dve", lambda e, p2=p2: e.scalar_tensor_tensor(
                    T.v(v_, TT), ps(p2), 1.0 / D, T.v(v_, TT), ALU.mult, ALU.subtract),
                    reads=psk(p2) + T.keys(v_, TT), writes=T.keys(v_, TT))
                P.add("act", lambda e: e.activation(T.v(v_, TT), T.v(v_, TT), AF.Ln, bias=EPSB.v(0, 1)),
                      reads=T.keys(v_, TT) + EPSB.keys(0, 1), writes=T.keys(v_, TT))
                P.add("act", lambda e: e.activation(T.v(v_, TT), T.v(v_, TT), AF.Exp, scale=-0.5),
                      reads=T.keys(v_, TT), writes=T.keys(v_, TT))
                P.add("dve", lambda e: e.scalar_tensor_tensor(
                    T.v(nb_, TT), T.v(m_, TT), -1.0, T.v(v_, TT), ALU.mult, ALU.mult),
                    reads=T.keys(m_, TT) + T.keys(v_, TT), writes=T.keys(nb_, TT))
                for c in range(KC):
                    uo = u_ + (c % 2) * TT
                    P.add("dve", lambda e, c=c, t=t, uo=uo: e.tensor_tensor(
                        T.v(uo, TT), B.v(c * S + t * TT, TT), T.v(v_, TT), ALU.mult),
                        reads=B.keys(c * S + t * TT, TT) + T.keys(v_, TT), writes=T.keys(uo, TT))
                    P.add("dve", lambda e, uo=uo: e.tensor_tensor(
                        T.v(uo, TT), T.v(uo, TT), T.v(nb_, TT), ALU.add),
                        reads=T.keys(uo, TT) + T.keys(nb_, TT), writes=T.keys(uo, TT))
                    P.add("act", lambda e, c=c, t=t, uo=uo: e.activation(
                        A.v(c * S + t * TT, TT), T.v(uo, TT), AF.Silu, bias=OV.v(32 + c, 1), scale=OV.v(24 + c, 1)),
                        reads=T.keys(uo, TT) + OV.keys(0, 48), writes=A.keys(c * S + t * TT, TT))

            ln_stats(0)
            ln_stats(1)
            ln_ew(0)
            for t in range(NT):
                if t + 2 < NT:
                    ln_stats(t + 2)
                if t + 1 < NT:
                    ln_ew(t + 1)
                out_proj(w_out, [(A, k * S) for k in range(KC)], C, 0, bias_col=40, tiles=(t,), load=False)
            reserved_banks.clear()


        def gen_tables():
            CF = _View(C, 0, F32)
            PI = math.pi
            MAGIC = 12582912.0
            C1 = 6.28125
            C2 = 2.0 * PI - 6.28125
            PIC = 3.1415925
            P.add("dve", lambda e: e.tensor_scalar(SM.v(6, 1), SM.v(3, 1), 1.0 / 32.0, -15.5 / 32.0, ALU.mult, ALU.add),
                  reads=SM.keys(3, 1), writes=SM.keys(6, 1))
            P.add("dve", lambda e: e.tensor_scalar(SM.v(6, 1), SM.v(6, 1), MAGIC, None, ALU.add),
                  reads=SM.keys(6, 1), writes=SM.keys(6, 1))
            P.add("dve", lambda e: e.tensor_scalar(SM.v(7, 1), SM.v(6, 1), -MAGIC, None, ALU.add),
                  reads=SM.keys(6, 1), writes=SM.keys(7, 1))
            P.add("dve", lambda e: e.scalar_tensor_tensor(SM.v(12, 1), SM.v(7, 1), -32.0, SM.v(3, 1), ALU.mult, ALU.add),
                  reads=SM.keys(7, 1) + SM.keys(3, 1), writes=SM.keys(12, 1))
            P.add("act", lambda e: e.activation(SM.v(4, 1), SM.v(12, 1), AF.Exp, scale=-math.log(10000.0) / 32.0),
                  reads=SM.keys(12, 1), writes=SM.keys(4, 1))
            NW = 2048
            for (dst, phase) in ((0, PI / 2.0), (2048, 0.0)):
                ck = CF.keys(dst, NW)
                tk = A.keys(0, NW, dt=F32)
                pos_a = IOTC.v(0, 128).unsqueeze(1).to_broadcast([128, 16, 128])
                pos_b = SM.v(16, 16).unsqueeze(2).to_broadcast([128, 16, 128])
                P.add("dve", lambda e, dst=dst, pos_a=pos_a, pos_b=pos_b: e.tensor_tensor(
                    CF.v(dst, NW).rearrange("p (b j) -> p b j", b=16), pos_a, pos_b, ALU.add),
                    reads=IOTC.keys(0, 128) + SM.keys(16, 16), writes=ck)
                P.add("dve", lambda e, dst=dst, phase=phase: e.tensor_scalar(
                    CF.v(dst, NW), CF.v(dst, NW), SM.v(4, 1), phase, ALU.mult, ALU.add),
                    reads=ck + SM.keys(4, 1), writes=ck)
                P.add("dve", lambda e, dst=dst: e.tensor_scalar(A.v(0, NW, dt=F32), CF.v(dst, NW), 1.0 / (2.0 * PI), MAGIC,
                                                                ALU.mult, ALU.add), reads=ck, writes=tk)
                P.add("dve", lambda e: e.tensor_scalar(A.v(0, NW, dt=F32), A.v(0, NW, dt=F32), -MAGIC, None, ALU.add),
                      reads=tk, writes=tk)
                for cc in (-C1, -C2):
                    P.add("dve", lambda e, dst=dst, cc=cc: e.scalar_tensor_tensor(
                        CF.v(dst, NW), A.v(0, NW, dt=F32), cc, CF.v(dst, NW), ALU.mult, ALU.add),
                        reads=ck + tk, writes=ck)
                P.add("dve", lambda e, dst=dst: e.tensor_scalar(CF.v(dst, NW), CF.v(dst, NW), PIC, -PIC, ALU.min, ALU.max),
                      reads=ck, writes=ck)
                sin_q.append(dst)
            tab_state[0] = True

        sin_q = []

        def gen_tables_finish():
            CF = _View(C, 0, F32)
            while sin_q:
                dst = sin_q.pop(0)
                P.add("act", lambda e, dst=dst: e.activation(CF.v(dst, 2048), CF.v(dst, 2048), SINF),
                      reads=CF.keys(dst, 2048), writes=CF.keys(dst, 2048))
                if not sin_q:
                    P.add("sp", lambda e: e.dma_start(out=tab_d, in_=CF.v(0, 4096)), reads=CF.keys(0, 4096),
                          writes=[("TAB", 0)], dma=True)

        def pre_even(l):
            i = l // 2
            linit = 0.8 - 0.6 * math.exp(-0.3 * l)
            for qi, (na, nb) in enumerate((("diff_lq1", "diff_lk1"), ("diff_lq2", "diff_lk2"))):
                def issue(na=na, nb=nb):
                    P.add("sp", lambda e: e.dma_start(out=VS.v(0, 64), in_=dr[na][i].partition_broadcast(128)),
                          writes=VS.keys(0, 64), dma=True)
                    P.add("sp", lambda e: e.dma_start(out=VS.v(64, 64), in_=dr[nb][i].partition_broadcast(128)),
                          writes=VS.keys(64, 64), dma=True)

                def consume(qi=qi):
                    P.add("dve", lambda e: e.tensor_tensor(VS.v(128, 64), VS.v(0, 64), VS.v(64, 64), ALU.mult),
                          reads=VS.keys(0, 128), writes=VS.keys(128, 64))
                    P.add("dve", lambda e: e.reduce_sum(SM.v(8 + qi, 1), VS.v(128, 64), mybir.AxisListType.X),
                          reads=VS.keys(128, 64), writes=SM.keys(8 + qi, 1))
                    P.add("act", lambda e: e.activation(SM.v(10 + qi, 1), SM.v(8 + qi, 1), AF.Exp),
                          reads=SM.keys(8 + qi, 1), writes=SM.keys(10 + qi, 1))
                stepq.append((issue, consume))

            def issue3():
                P.add("sp", lambda e: e.dma_start(out=SM.v(2, 1), in_=dr["diff_subln_g"][i].rearrange("(p o) -> p o", o=1)),
                      writes=SM.keys(2, 1), dma=True)

            def consume3():
                P.add("dve", lambda e: e.scalar_tensor_tensor(SM.v(0, 1), SM.v(11, 1), -linit, SM.v(10, 1),
                                                              ALU.add, ALU.subtract),
                      reads=SM.keys(10, 2), writes=SM.keys(0, 1))
                P.add("dve", lambda e: e.tensor_scalar(SM.v(1, 1), SM.v(2, 1), 1.0 - linit, None, ALU.mult),
                      reads=SM.keys(2, 1), writes=SM.keys(1, 1))
            stepq.append((issue3, consume3))

        def even_mixer(l):
            i = l // 2
            linit = 0.8 - 0.6 * math.exp(-0.3 * l)
            w_in = dr["ab_w_in"][i]
            w_out = dr["ab_w_out"][i]
            CF = _View(C, 0, F32)
            if STAGE < 1:
                return
            rmsnorm_to_H(l)
            if mixer_norm_announce[0] is not None:
                norm_prepare(mixer_norm_announce[0], False)
            run_hook()
            if not tab_gen[0]:
                gen_tables()
                tab_gen[0] = True
            elif not tab_state[0]:
                P.add("sp", lambda e: e.dma_start(out=CF.v(0, 4096), in_=tab_d), reads=[("TAB", 0)],
                      writes=CF.keys(0, 4096), dma=True)
            tab_state[0] = False
            if STAGE < 3:
                return
            def tok_major(col0, consume):
                for half in range(2):
                    so = ((half + 1) % 3) * 2048
                    slot = W.v3(so, KC, 256)
                    wload(slot, W.keys(so, 2048),
                          w_in[:, col0 + half * 256:col0 + (half + 1) * 256].rearrange("(k p) n -> p k n", p=128))
                for blk in range(16):
                    pb = bank()
                    for half in range(2):
                        so = ((half + 1) % 3) * 2048
                        slot = W.v3(so, KC, 256)
                        for k in range(KC):
                            P.add("pe", lambda e, k=k, blk=blk, pb=pb, half=half, slot=slot: e.matmul(
                                ps(pb, 256, half * 256), H.v(k * S + blk * 128, 128), slot[:, k, :],
                                start=(k == 0), stop=(k == KC - 1)),
                                reads=W.keys(so, 2048) + H.keys(k * S + blk * 128, 128), writes=psk(pb))
                    consume(blk, pb)

            def v_consume(blk, pb):
                P.add("act", lambda e: e.copy(B.v(blk * 512, 512), ps(pb)),
                      reads=psk(pb), writes=B.keys(blk * 512, 512))
            tok_major(1024, v_consume)
            for g in range(4):
                so = (g % 3) * 2048
                slot = W.v3(so, KC, 256)
                wload(slot[:, :, 0:128], W.keys(so, 2048),
                      w_in[:, 1536 + g * 128:1536 + (g + 1) * 128].rearrange("(k p) n -> p k n", p=128))
                for t in range(NT):
                    pb = bank()
                    proj_fm(slot, so, 0, t, pb)
                    P.add("act", lambda e, pb=pb, g=g, t=t: e.activation(
                        B.v(8192 + g * S + t * TT, TT), ps(pb), AF.Gelu_apprx_tanh),
                        reads=psk(pb), writes=B.keys(8192 + g * S + t * TT, TT))
            gen_tables_finish()
            if STAGE < 2:
                return
            rope_pending = []

            def rope_flush():
                while rope_pending:
                    (qc, t, pb, rq, ri) = rope_pending.pop(0)
                    pr = bank()
                    P.add("pe", lambda e, pr=pr, rq=rq: e.matmul(ps(pr), RM.v(0, 128), Q.v(rq, TT), start=True, stop=True),
                          reads=RM.keys(0, 128) + Q.keys(rq, TT), writes=psk(pr))
                    t1 = ((ri % 2) * 2) * TT
                    t2 = t1 + TT
                    P.add("dve", lambda e, pb=pb, t1=t1, t=t: e.tensor_tensor(T.v(t1, TT), ps(pb), CF.v(t * TT, TT), ALU.mult),
                          reads=psk(pb) + CF.keys(t * TT, TT), writes=T.keys(t1, TT))
                    P.add("dve", lambda e, pr=pr, t2=t2, t=t: e.tensor_tensor(T.v(t2, TT), ps(pr), CF.v(2048 + t * TT, TT), ALU.mult),
                          reads=psk(pr) + CF.keys(2048 + t * TT, TT), writes=T.keys(t2, TT))
                    P.add("dve", lambda e, t1=t1, t2=t2, qc=qc, t=t: e.tensor_tensor(
                        A.v(qc * S + t * TT, TT), T.v(t1, TT), T.v(t2, TT), ALU.add),
                        reads=T.keys(t1, TT) + T.keys(t2, TT), writes=A.keys(qc * S + t * TT, TT))

            ri = 0
            for qc in range(8):
                so = (qc % 3) * 2048
                slot = W.v3(so, KC, 256)
                wload(slot[:, :, 0:128], W.keys(so, 2048),
                      w_in[:, qc * 128:(qc + 1) * 128].rearrange("(k p) n -> p k n", p=128))
                for t in range(NT):
                    pb = bank()
                    proj_fm(slot, so, 0, t, pb)
                    rq = (ri % 2) * TT
                    P.add("act", lambda e, pb=pb, rq=rq: e.copy(Q.v(rq, TT), ps(pb)),
                          reads=psk(pb), writes=Q.keys(rq, TT))
                    rope_flush()
                    rope_pending.append((qc, t, pb, rq, ri))
                    ri += 1
                tick()
            rope_flush()
            ZS = 4 * TT
            X_ = mybir.AxisListType.X

            def z_consume(blk, pb):
                zt = (blk % 2) * TT
                sqt = 2 * TT + (blk % 2) * TT
                P.add("act", lambda e: e.activation(T.v(zt, TT), ps(pb), AF.Gelu_apprx_tanh),
                      reads=psk(pb), writes=T.keys(zt, TT))
                P.add("act", lambda e: e.copy(C.v(blk * 512, 512), T.v(zt, TT)),
                      reads=T.keys(zt, TT), writes=C.keys(blk * 512, 512))
                P.add("dve", lambda e: e.tensor_reduce(T.v(ZS + blk * 4, 4), T.v3(zt, 4, 128), X_, ALU.add),
                      reads=T.keys(zt, TT), writes=T.keys(ZS, TT))
                P.add("act", lambda e: e.activation(T.v(sqt, TT), T.v(zt, TT), AF.Square),
                      reads=T.keys(zt, TT), writes=T.keys(sqt, TT))
                P.add("dve", lambda e: e.tensor_reduce(T.v(ZS + 64 + blk * 4, 4), T.v3(sqt, 4, 128), X_, ALU.add),
                      reads=T.keys(sqt, TT), writes=T.keys(ZS, TT))
            tok_major(2048, z_consume)
            zk = T.keys(ZS, TT)
            P.add("dve", lambda e: e.tensor_scalar(T.v(ZS + 128, 64), T.v(ZS, 64), 1.0 / 128.0, None, ALU.mult),
                  reads=zk, writes=zk)
            P.add("dve", lambda e: e.tensor_tensor(T.v(ZS + 256, 64), T.v(ZS + 128, 64), T.v(ZS + 128, 64), ALU.mult),
                  reads=zk, writes=zk)
            P.add("dve", lambda e: e.scalar_tensor_tensor(T.v(ZS + 192, 64), T.v(ZS + 64, 64), 1.0 / 128.0, T.v(ZS + 256, 64),
                                                          ALU.mult, ALU.subtract),
                  reads=zk, writes=zk)
            P.add("act", lambda e: e.activation(T.v(ZS + 192, 64), T.v(ZS + 192, 64), AF.Ln, bias=EPSB.v(0, 1)),
                  reads=zk + EPSB.keys(0, 1), writes=zk)
            P.add("act", lambda e: e.activation(T.v(ZS + 192, 64), T.v(ZS + 192, 64), AF.Exp, scale=-0.5),
                  reads=zk, writes=zk)
            LNG, LNB = 4 * TT, 5 * TT
            P.add("pool", lambda e: e.dma_start(out=T.v(LNG, TT, dt=BF16), in_=dr["sg_ln_g"][i].partition_broadcast(128)),
                  writes=T.keys(LNG, TT, dt=BF16), dma=True)
            P.add("pool", lambda e: e.dma_start(out=T.v(LNB, TT, dt=BF16), in_=dr["sg_ln_b"][i].partition_broadcast(128)),
                  writes=T.keys(LNB, TT, dt=BF16), dma=True)
            z2_list = []

            def z_pass2(blk):
                zt = 3 * TT
                mean_b = T.v(ZS + 128 + blk * 4, 4).unsqueeze(2).to_broadcast([128, 4, 128])
                rstd_b = T.v(ZS + 192 + blk * 4, 4).unsqueeze(2).to_broadcast([128, 4, 128])
                P.add("dve", lambda e, blk=blk, zt=zt, mean_b=mean_b: e.tensor_tensor(
                    T.v3(zt, 4, 128), C.v3(blk * 512, 4, 128), mean_b, ALU.subtract),
                    reads=C.keys(blk * 512, 512) + zk, writes=T.keys(zt, TT))
                P.add("dve", lambda e, zt=zt, rstd_b=rstd_b: e.tensor_tensor(
                    T.v3(zt, 4, 128), T.v3(zt, 4, 128), rstd_b, ALU.mult),
                    reads=T.keys(zt, TT) + zk, writes=T.keys(zt, TT))
                P.add("dve", lambda e, zt=zt: e.tensor_tensor(T.v(zt, TT), T.v(zt, TT), T.v(LNG, TT, dt=BF16), ALU.mult),
                      reads=T.keys(zt, TT) + T.keys(LNG, TT, dt=BF16), writes=T.keys(zt, TT))
                P.add("dve", lambda e, zt=zt, blk=blk: e.tensor_tensor(C.v(blk * 512, 512), T.v(zt, TT), T.v(LNB, TT, dt=BF16), ALU.add),
                      reads=T.keys(zt, TT) + T.keys(LNB, TT, dt=BF16), writes=C.keys(blk * 512, 512))
            z2_list.extend(range(16))
            if STAGE < 4:
                return
            scale = 64.0 ** -0.5
            tasks = []
            for h in range(4):
                for qt in range(NT):
                    for kb in range(4 * qt + 4):
                        tasks.append((h, qt, kb))

            def geom(task):
                h, qt, kb = task
                di = kb - 4 * qt
                lo = 128 * di if di > 0 else 0
                return h, qt, kb, di, lo, TT - lo, 4 * qt + 4

            def emit_qk(idx):
                h, qt, kb, di, lo, nn, nkb = geom(tasks[idx])
                sbk = 4 + 2 * (idx % 2)
                for c in range(2):
                    P.add("pe", lambda e, c=c: e.matmul(
                        ps(sbk + c, nn, lo), A.v((4 + h) * S + kb * 128, 128, 64 * c, 64 * c + 64),
                        A.v(h * S + qt * TT + lo, nn, 64 * c, 64 * c + 64), start=True, stop=(di < 0)),
                        reads=A.keys((4 + h) * S + kb * 128, 128) + A.keys(h * S + qt * TT + lo, nn),
                        writes=psk(sbk + c))
                    if di >= 0:
                        for (so_, mo_, last) in ((0, 256, False), (128, 384, True)):
                            P.add("pe", lambda e, c=c, so_=so_, mo_=mo_, last=last: e.matmul(
                                ps(sbk + c, 128, lo), CM.v(so_, 128, 64 * c, 64 * c + 64), CM.v(mo_, 128, 64 * c, 64 * c + 64),
                                start=False, stop=last),
                                reads=CM.keys(0, 512), writes=psk(sbk + c))

            def emit_rest(idx):
                h, qt, kb, di, lo, nn, nkb = geom(tasks[idx])
                sbk = 4 + 2 * (idx % 2)
                pp = ((2 * idx) % 6) * TT
                pos = [pp, pp + TT]
                P.add("act", lambda e: e.activation(
                    H.v3(pp, 2, TT)[:, :, lo:TT], PS[:, sbk:sbk + 2, lo:TT], AF.Exp, scale=scale),
                    reads=psk(sbk) + psk(sbk + 1), writes=H.keys(pp, 2 * TT))
                for c in range(2):
                    po = pos[c]
                    P.add("pe", lambda e, c=c, po=po: e.matmul(
                        ps(c, nn, lo), B.v(kb * 512 + h * 128, 128), H.v(po + lo, nn),
                        start=(kb == 0), stop=(kb == nkb - 1)),
                        reads=B.keys(kb * 512 + h * 128, 128) + H.keys(po, TT), writes=psk(c))
                    P.add("pe", lambda e, c=c, po=po: e.matmul(
                        ps(2 + c, nn, lo), ONES.v(0, 128), H.v(po + lo, nn),
                        start=(kb == 0), stop=(kb == nkb - 1)),
                        reads=ONES.keys(0, 128) + H.keys(po, TT), writes=psk(2 + c))
                if kb == nkb - 1:
                    fb = 2048 + ((h * NT + qt) % 2) * 2048
                    r1, r2, o1, o2 = fb, fb + TT, fb + 2 * TT, fb + 3 * TT
                    for (rr, bk) in ((r1, 2), (r2, 3)):
                        P.add("act", lambda e, rr=rr, bk=bk: e.activation(H.v(rr, TT, dt=F32), ps(bk), AF.Ln),
                              reads=psk(bk), writes=H.keys(rr, TT, dt=F32))
                    for (oo, bk) in ((o1, 0), (o2, 1)):
                        P.add("dve", lambda e, oo=oo, bk=bk: e.tensor_copy(H.v(oo, TT, dt=F32), ps(bk)),
                              reads=psk(bk), writes=H.keys(oo, TT, dt=F32))
                    for rr in (r1, r2):
                        P.add("act", lambda e, rr=rr: e.activation(H.v(rr, TT, dt=F32), H.v(rr, TT, dt=F32), AF.Exp, scale=-1.0),
                              reads=H.keys(rr, TT, dt=F32), writes=H.keys(rr, TT, dt=F32))
                    for (oo, rr) in ((o1, r1), (o2, r2)):
                        P.add("dve", lambda e, oo=oo, rr=rr: e.tensor_tensor(
                            H.v(oo, TT, dt=F32), H.v(oo, TT, dt=F32), H.v(rr, TT, dt=F32), ALU.mult),
                            reads=H.keys(oo, TT, dt=F32) + H.keys(rr, TT, dt=F32), writes=H.keys(oo, TT, dt=F32))
                    P.add("dve", lambda e: e.scalar_tensor_tensor(
                        A.v(h * S + qt * TT, TT), H.v(o2, TT, dt=F32), SM.v(0, 1), H.v(o1, TT, dt=F32), ALU.mult, ALU.add),
                        reads=H.keys(o1, TT, dt=F32) + H.keys(o2, TT, dt=F32) + SM.keys(0, 1), writes=A.keys(h * S + qt * TT, TT))
                    subln_q.append((idx + 3, h, qt))

            def subln(h, qt, cur):
                it = h * NT + qt
                ao = h * S + qt * TT
                sq = (it % 2) * TT
                tr = TT
                P.add("act", lambda e: e.activation(Q.v(sq, TT), A.v(ao, TT), AF.Square),
                      reads=A.keys(ao, TT), writes=Q.keys(sq, TT))
                pss = 4 + 2 * (cur % 2)
                P.add("pe", lambda e: e.matmul(ps(pss), ONES.v(0, 128), Q.v(sq, TT), start=True, stop=True),
                      reads=ONES.keys(0, 128) + Q.keys(sq, TT), writes=psk(pss))
                P.add("act", lambda e: e.activation(T.v(tr, TT), ps(pss), AF.Ln, bias=EPSB.v(0, 1), scale=1.0 / 128.0),
                      reads=psk(pss) + EPSB.keys(0, 1), writes=T.keys(tr, TT))
                P.add("act", lambda e: e.activation(T.v(tr, TT), T.v(tr, TT), AF.Exp, scale=-0.5),
                      reads=T.keys(tr, TT), writes=T.keys(tr, TT))
                P.add("dve", lambda e: e.scalar_tensor_tensor(
                    A.v(ao, TT), A.v(ao, TT), SM.v(1, 1), T.v(tr, TT), ALU.mult, ALU.mult),
                    reads=A.keys(ao, TT) + T.keys(tr, TT) + SM.keys(1, 1), writes=A.keys(ao, TT))

            subln_q = []
            emit_qk(0)
            for idx in range(len(tasks)):
                if idx + 1 < len(tasks):
                    emit_qk(idx + 1)
                emit_rest(idx)
                if z2_list and idx % 8 == 4:
                    z_pass2(z2_list.pop(0))
                if idx % 8 == 0:
                    tick()
                while subln_q and subln_q[0][0] <= idx:
                    _, hh, qq = subln_q.pop(0)
                    subln(hh, qq, idx)
            while z2_list:
                z_pass2(z2_list.pop(0))
            while subln_q:
                _, hh, qq = subln_q.pop(0)
                subln(hh, qq, len(tasks) - 1)
            if STAGE < 5:
                return
            SGW = 3072
            P.add("sp", lambda e: e.dma_start(out=VS.v3(0, 4, 128), in_=dr["sg_w"][i].rearrange("g t s -> t g s")),
                  writes=VS.keys(0, 512), dma=True)
            pb = bank()
            for g in range(4):
                P.add("pe", lambda e, g=g, pb=pb: e.transpose(ps(pb, 128, g * 128), VS.v(g * 128, 128), IDF.v(0, 128)),
                      reads=VS.keys(0, 512) + IDF.keys(0, 128), writes=psk(pb))
            for g in range(4):
                P.add("dve", lambda e, g=g, pb=pb: e.tensor_tensor(H.v(SGW + g * 128, 128), ps(pb, 128, g * 128),
                                                                    MASK.v(0, 128), ALU.mult),
                      reads=psk(pb) + MASK.keys(0, 128), writes=H.keys(SGW + g * 128, 128))
            BT = 4 * TT
            P.add("sp", lambda e: e.dma_start(out=T.v(BT, TT), in_=dr["sg_b"][i].rearrange("g t -> (g t)").partition_broadcast(128)),
                  writes=T.keys(BT, TT), dma=True)
            for g in range(4):
                for tt in range(NT):
                    pb = bank()
                    for cc in range(4):
                        blk = tt * 4 + cc
                        P.add("pe", lambda e, g=g, cc=cc, blk=blk, pb=pb: e.matmul(
                            ps(pb, 128, cc * 128), C.v(blk * 512 + g * 128, 128), H.v(SGW + g * 128, 128),
                            start=True, stop=True),
                            reads=C.keys(blk * 512 + g * 128, 128) + H.keys(SGW + g * 128, 128), writes=psk(pb))
                    tz = ((g * NT + tt) % 2) * TT
                    bt_b = T.v(BT + g * 128, 128).unsqueeze(1).to_broadcast([128, 4, 128])
                    P.add("dve", lambda e, pb=pb, tz=tz, bt_b=bt_b: e.tensor_tensor(
                        T.v3(tz, 4, 128), PS[:, pb, :].rearrange("p (c t) -> p c t", c=4), bt_b, ALU.add),
                        reads=psk(pb) + T.keys(BT, TT), writes=T.keys(tz, TT))
                    uo = 8192 + g * S + tt * TT
                    P.add("dve", lambda e, tz=tz, uo=uo: e.tensor_tensor(B.v(uo, TT), T.v(tz, TT), B.v(uo, TT), ALU.mult),
                          reads=T.keys(tz, TT) + B.keys(uo, TT), writes=B.keys(uo, TT))
            out_proj(w_out, [(A, k * S) for k in range(4)] + [(B, 8192 + g * S) for g in range(4)], B, 0)

        consts()
        P.add("dve", lambda e: e.memset(EPSB.v(0, 8), EPS), writes=EPSB.keys(0, 8))
        load_rows_T(GV, 0, [dr["norm_mix_g"][i] for i in range(4)], 8)
        load_rows_T(GV, 32, [dr["norm_ffn_g"][i] for i in range(4)], 8)
        load_rows_T(GV, 64, [dr["final_norm_g"]], 8)
        load_x()
        mixer_norm_announce = [None]
        ffn_norm_announce = [None]

        def pre_mixer(l):
            if l % 2 == 0:
                pre_even(l)
            else:
                pre_odd(l)

        if do_mixer and len(layers) > 0:
            pre_mixer(layers[0])
            drain()
        for li, l in enumerate(layers):
            if do_mixer:
                if do_ffn:
                    hook[0] = (lambda l=l: pre_ffn(l))
                    norm_after_mixer = 4 + l
                else:
                    norm_after_mixer = None
                mixer_norm_announce[0] = norm_after_mixer
                if l % 2 == 0:
                    even_mixer(l)
                else:
                    odd_mixer(l)
                run_hook()
                drain()
            elif do_ffn:
                pre_ffn(l)
                drain()
            if do_ffn:
                if do_mixer and li + 1 < len(layers):
                    hook[0] = (lambda nl=layers[li + 1]: pre_mixer(nl))
                if li + 1 < len(layers):
                    ffn_norm_announce[0] = (layers[li + 1], False) if do_mixer else (4 + layers[li + 1], False)
                elif final_norm:
                    ffn_norm_announce[0] = (8, True)
                else:
                    ffn_norm_announce[0] = None
                ffn(l)
                run_hook()
                drain()
        if final_norm:
            final_rmsnorm_inplace(8)
        store_y(X)
        P.emit(nc, es)
    return nc


_NAMES = ["norm_mix_g", "norm_ffn_g", "ab_w_in", "ab_w_out", "diff_lq1", "diff_lk1", "diff_lq2",
          "diff_lk2", "diff_subln_g", "sg_ln_g", "sg_ln_b", "sg_w", "sg_b", "conv_w_in", "conv_b_in",
          "conv_dw_w", "conv_dw_b", "conv_ln_g", "conv_ln_b", "conv_w_out", "conv_b_out", "ffn_w_up",
          "ffn_dw_w", "ffn_dw_b", "ffn_w_down", "final_norm_g"]


def run(inputs, ncores=8, **bkw):
    import time
    t0 = time.time()
    nc = build_program(**bkw)
    print("build_program s:", time.time() - t0, flush=True)
    shared = {k: np.ascontiguousarray(np.asarray(inputs[k], dtype=np.float32)) for k in _NAMES}
    x = np.asarray(inputs["x"], dtype=np.float32)
    in_maps = []
    for i in range(ncores):
        m = dict(shared)
        m["x"] = np.ascontiguousarray(x[i])
        in_maps.append(m)
    t0 = time.time()
    res = run_bass_kernel_spmd(nc, in_maps, core_ids=list(range(ncores)))
    print("run_bass_kernel_spmd s:", time.time() - t0, flush=True)
    return np.stack([np.asarray(r["y"], dtype=np.float32) for r in res.results], axis=0)


def kernel(**inputs):
    return run(inputs, ncores=8)
```

```python
import math
from contextlib import ExitStack

import numpy as np
import concourse.bass as bass
import concourse.mybir as mybir
from concourse.bass_utils import run_bass_kernel_spmd

F32 = mybir.dt.float32
BF16 = mybir.dt.bfloat16
I32 = mybir.dt.int32
ALU = mybir.AluOpType
AF = mybir.ActivationFunctionType

S = 2048
D = 1024
NT = 4
TT = 512
KC = 8
DEPTH = 4
EPS = 1e-6
FH = 2816
FC = 22
GRAN = 1024
COMPUTE = ("pe", "act", "dve", "pool")
N_DMA_SEMS = 28
N_SP_SEMS = 12
SPLIT_POOLS = True
FULL_SYNC = True
NEAR_SYNC = 4
SINF = AF.Sin
STAGE = 99


class Op:
    __slots__ = ("eng", "fn", "reads", "writes", "dma", "waits", "signal", "count",
                 "pos", "snap", "snap_dma", "dsem", "dval", "idx")


class Prog:
    def __init__(self):
        self.ops = []

    def add(self, eng, fn, reads=(), writes=(), dma=False):
        o = Op()
        o.eng = eng
        o.fn = fn
        o.reads = tuple(reads)
        o.writes = tuple(writes)
        o.dma = dma
        o.waits = []
        o.signal = False
        o.count = 0
        o.pos = -1
        o.idx = len(self.ops)
        self.ops.append(o)
        return o

    def analyse(self):
        ops = self.ops
        last_w = {}
        readers = {}
        engs = COMPUTE + ("sp",)
        known = {e: {f: -1 for f in COMPUTE} for e in engs}
        known_dma = {e: frozenset() for e in engs}
        eng_pos = {e: 0 for e in COMPUTE}
        dma_idx = {"sp": 0, "pool": 0}
        dma_hist = {}
        for i, op in enumerate(ops):
            E = op.eng
            deps = set()
            for r in op.reads:
                w = last_w.get(r)
                if w is not None:
                    deps.add(w)
                if r[0] == "PS":
                    for rd in readers.get(r, ()):
                        if ops[rd].eng != E:
                            deps.add(rd)
            for r in op.writes:
                w = last_w.get(r)
                if w is not None:
                    deps.add(w)
                for rd in readers.get(r, ()):
                    deps.add(rd)
            if op.dma:
                if SPLIT_POOLS:
                    base, npool = (0, N_SP_SEMS) if E == "sp" else (N_SP_SEMS, N_DMA_SEMS - N_SP_SEMS)
                    di = dma_idx[E]
                else:
                    base, npool = 0, 24
                    di = dma_idx["sp"] + dma_idx["pool"]
                op.dsem = base + di % npool
                op.dval = 16 * (di // npool + 1)
                prev = dma_hist.get(op.dsem)
                if prev is not None:
                    deps.add(prev)
                dma_hist[op.dsem] = i
                dma_idx[E] += 1
            deps.discard(i)
            kn = known[E]
            kd = known_dma[E]
            wset = None
            for j in sorted(deps):
                oj = ops[j]
                if oj.dma:
                    if j in kd:
                        continue
                    op.waits.append(j)
                    kd = kd | oj.snap_dma | frozenset((j,))
                    for f, v in oj.snap.items():
                        if v > kn[f]:
                            kn[f] = v
                else:
                    F = oj.eng
                    if F == E and E == "pe":
                        continue
                    if F == E and not FULL_SYNC:
                        if wset is None:
                            wset = set(op.reads)
                        if not any(r in wset for r in oj.writes) and eng_pos[E] - oj.pos > NEAR_SYNC:
                            continue
                    if kn[F] >= oj.pos:
                        continue
                    op.waits.append(j)
                    kn[F] = oj.pos
                    for f, v in oj.snap.items():
                        if v > kn[f]:
                            kn[f] = v
                    kd = kd | oj.snap_dma
            known_dma[E] = kd
            op.snap = dict(kn)
            op.snap_dma = kd
            for r in op.reads:
                readers.setdefault(r, []).append(i)
            for r in op.writes:
                last_w[r] = i
                readers[r] = []
            if not op.dma and E in eng_pos:
                op.pos = eng_pos[E]
                eng_pos[E] += 1
        for op in ops:
            for j in op.waits:
                if not ops[j].dma:
                    ops[j].signal = True
        cnt = {e: 0 for e in COMPUTE}
        for op in ops:
            if op.signal:
                cnt[op.eng] += 1
                op.count = cnt[op.eng]

    def emit(self, nc, es):
        self.analyse()
        ops = self.ops
        esem = {e: es.enter_context(nc.semaphore("s_" + e)) for e in COMPUTE}
        dsem = [es.enter_context(nc.semaphore("d_%d" % i)) for i in range(N_DMA_SEMS)]
        per_eng = {e: [] for e in COMPUTE + ("sp",)}
        for op in ops:
            per_eng[op.eng].append(op)

        def run(name, e):
            for op in per_eng[name]:
                for j in op.waits:
                    oj = ops[j]
                    if oj.dma:
                        e.wait_ge(dsem[oj.dsem], oj.dval)
                    else:
                        e.wait_ge(esem[oj.eng], oj.count)
                if op.fn is None:
                    continue
                ins = op.fn(e)
                if op.dma:
                    ins.then_inc(dsem[op.dsem], 16)
                elif op.signal:
                    ins.then_inc(esem[op.eng], 1)

        with nc.Block() as block:
            @block.tensor
            def _(e):
                run("pe", e)

            @block.scalar
            def _(e):
                run("act", e)

            @block.vector
            def _(e):
                run("dve", e)

            @block.gpsimd
            def _(e):
                run("pool", e)

            @block.sync
            def _(e):
                run("sp", e)


class Buf:
    def __init__(self, nc, es, name, n, dtype, parts=128):
        self.name = name
        self.n = n
        self.dtype = dtype
        self.esz = 4 if dtype in (F32, I32) else 2
        self.t = es.enter_context(nc.sbuf_tensor(name, [parts, n], dtype))

    def keys(self, off, n, dt=None):
        sz = self.esz if dt is None else (4 if dt in (F32, I32) else 2)
        b0 = off * sz
        b1 = (off + n) * sz - 1
        return [(self.name, g) for g in range(b0 // GRAN, b1 // GRAN + 1)]

    def v(self, off, n, p0=0, p1=128, dt=None):
        if dt is None or dt == self.dtype:
            return self.t[p0:p1, off:off + n]
        sz = 4 if dt in (F32, I32) else 2
        b0 = off * sz // self.esz
        bn = n * sz // self.esz
        return self.t[p0:p1, b0:b0 + bn].bitcast(dt)

    def v3(self, off, k, n, p0=0, p1=128, dt=None):
        return self.v(off, k * n, p0, p1, dt).rearrange("p (k n) -> p k n", k=k)


def build_program(layers=(0, 1, 2, 3), final_norm=True, do_mixer=True, do_ffn=True):
    nc = bass.Bass("TRN2", target_bir_lowering=False)
    dr = {}

    def dram(name, shape, kind="ExternalInput"):
        dr[name] = nc.dram_tensor(name, list(shape), F32, kind=kind).ap()
        return dr[name]

    x_d = dram("x", [S, D])
    dram("norm_mix_g", [4, D])
    dram("norm_ffn_g", [4, D])
    dram("ab_w_in", [2, D, 2560])
    dram("ab_w_out", [2, D, D])
    for nm in ("diff_lq1", "diff_lk1", "diff_lq2", "diff_lk2"):
        dram(nm, [2, 64])
    dram("diff_subln_g", [2, 128])
    dram("sg_ln_g", [2, 512])
    dram("sg_ln_b", [2, 512])
    dram("sg_w", [2, 4, 128, 128])
    dram("sg_b", [2, 4, 128])
    dram("conv_w_in", [2, D, 2 * D])
    dram("conv_b_in", [2, 2 * D])
    dram("conv_dw_w", [2, 31, D])
    dram("conv_dw_b", [2, D])
    dram("conv_ln_g", [2, D])
    dram("conv_ln_b", [2, D])
    dram("conv_w_out", [2, D, D])
    dram("conv_b_out", [2, D])
    dram("ffn_w_up", [4, D, 2 * FH])
    dram("ffn_dw_w", [4, 3, 2 * FH])
    dram("ffn_dw_b", [4, 2 * FH])
    dram("ffn_w_down", [4, FH, D])
    dram("final_norm_g", [D])
    y_d = dram("y", [S, D], kind="ExternalOutput")
    tab_d = nc.dram_tensor("rope_tab", [128, 4096], F32, kind="Internal").ap()
    tab_state = [False]
    tab_gen = [False]

    P = Prog()
    es = ExitStack()
    with es:
        X = Buf(nc, es, "X", KC * S, F32)
        H = Buf(nc, es, "H", KC * S, BF16)
        A = Buf(nc, es, "A", KC * S, BF16)
        B = Buf(nc, es, "B", KC * S, BF16)
        C = Buf(nc, es, "C", 8192, BF16)
        W = Buf(nc, es, "W", 6144, BF16)
        T = Buf(nc, es, "T", 5 * TT, F32)
        Q = Buf(nc, es, "Q", 4 * TT, BF16)
        IDB = Buf(nc, es, "IDB", 128, BF16)
        RM = Buf(nc, es, "RM", 128, BF16)
        OV = Buf(nc, es, "OV", 48, F32)
        DWT = Buf(nc, es, "DWT", 248, F32)
        SM = Buf(nc, es, "SM", 64, F32)
        MASK = Buf(nc, es, "MASK", 128, BF16)
        IOTC = Buf(nc, es, "IOTC", 128, F32)
        CM = Buf(nc, es, "CM", 512, BF16)
        IDF = Buf(nc, es, "IDF", 128, F32)
        ONES = Buf(nc, es, "ONES", 128, BF16)
        GV = Buf(nc, es, "GV", 9 * 8, F32)
        FD = Buf(nc, es, "FD", 4 * 44, F32)
        PS = es.enter_context(nc.psum_tensor("PS", [128, 8, TT], F32))

        class _View:
            def __init__(self, buf, base, dt):
                self.buf, self.base, self.dt = buf, base, dt

            def v(self, off, n, p0=0, p1=128):
                return self.buf.v(self.base + off, n, p0, p1, dt=self.dt)

            def keys(self, off, n):
                return self.buf.keys(self.base + off, n, dt=self.dt)

        VS = _View(Q, 512, F32)
        VS.v3 = lambda off, k, n, p0=0, p1=128: Q.v3(512 + off, k, n, p0, p1, dt=F32)
        IOT = _View(T, 0, F32)
        TI = _View(T, 0, I32)
        SMI = _View(SM, 32, I32)
        bank_ctr = [0]

        reserved_banks = set()

        def bank():
            while True:
                b = bank_ctr[0] % 8
                bank_ctr[0] += 1
                if b not in reserved_banks:
                    return b

        def psk(b):
            return [("PS", b)]

        def ps(b, n=TT, off=0, p0=0, p1=128):
            return PS[p0:p1, b, off:off + n]

        def consts():
            P.add("pool", lambda e: e.iota(IOT.v(0, 128), [[1, 128]], base=0, channel_multiplier=-1,
                                           allow_small_or_imprecise_dtypes=True),
                  writes=IOT.keys(0, 128))
            P.add("pool", lambda e: e.iota(IOTC.v(0, 128), [[1, 128]], base=0, channel_multiplier=0,
                                           allow_small_or_imprecise_dtypes=True),
                  writes=IOTC.keys(0, 128))
            P.add("pool", lambda e: e.iota(SM.v(3, 1), [[0, 1]], base=0, channel_multiplier=1,
                                           allow_small_or_imprecise_dtypes=True),
                  writes=SM.keys(3, 1))
            P.add("pool", lambda e: e.iota(SM.v(16, 16), [[128, 16]], base=0, channel_multiplier=0,
                                           allow_small_or_imprecise_dtypes=True),
                  writes=SM.keys(16, 16))
            P.add("dve", lambda e: e.tensor_single_scalar(IDF.v(0, 128), IOT.v(0, 128), 0.0, ALU.is_equal),
                  reads=IOT.keys(0, 128) + IOTC.keys(0, 128) + SM.keys(0, 64), writes=IDF.keys(0, 128))
            P.add("dve", lambda e: e.memset(ONES.v(0, 128), 1.0), writes=ONES.keys(0, 128))
            P.add("dve", lambda e: e.tensor_single_scalar(IDB.v(0, 128), IOT.v(0, 128), 0.0, ALU.is_equal),
                  reads=IOT.keys(0, 128), writes=IDB.keys(0, 128))
            P.add("dve", lambda e: e.tensor_single_scalar(MASK.v(0, 128), IOT.v(0, 128), 0.0, ALU.is_ge),
                  reads=IOT.keys(0, 128), writes=MASK.keys(0, 128))
            NEG = -30000.0
            for (off, op, thr_lo, thr_hi, mul) in ((0, ALU.is_equal, 0.0, -64.0, 1.0), (128, ALU.is_equal, 64.0, 0.0, 1.0),
                                                   (256, ALU.is_lt, 0.0, -64.0, NEG), (384, ALU.is_lt, 64.0, 0.0, NEG)):
                for (p0, thr) in ((0, thr_lo), (64, thr_hi)):
                    P.add("dve", lambda e, off=off, op=op, p0=p0, thr=thr, mul=mul: e.tensor_scalar(
                        CM.v(off, 128, p0, p0 + 64), IOT.v(0, 128, p0, p0 + 64), thr, mul, op, ALU.mult),
                        reads=IOT.keys(0, 128), writes=CM.keys(0, 512))
            for (d0, s0, sgn) in ((0, 32, -1.0), (32, 0, 1.0), (64, 96, -1.0), (96, 64, 1.0)):
                P.add("dve", lambda e, d0=d0, s0=s0, sgn=sgn: e.tensor_scalar(
                    RM.v(d0, 32), IDB.v(s0, 32), sgn, None, ALU.mult),
                    reads=IDB.keys(0, 128), writes=RM.keys(0, 128))

        def load_rows_T(dst_buf, dst_off, rows, Cn):
            R = len(rows)
            assert R * 128 <= 512 and R * Cn <= TT
            for r, row in enumerate(rows):
                P.add("sp", lambda e, r=r, row=row: e.dma_start(
                    out=VS.v(r * 128, 128, 0, Cn), in_=row.rearrange("(c p) -> c p", p=128)),
                    writes=VS.keys(r * 128, 128), dma=True)
            pb = bank()
            for r in range(R):
                P.add("pe", lambda e, r=r: e.transpose(ps(pb, Cn, r * Cn), VS.v(r * 128, 128, 0, Cn),
                                                       IDF.v(0, Cn, 0, Cn)),
                      reads=VS.keys(r * 128, 128) + IDF.keys(0, 128), writes=psk(pb))
            P.add("dve", lambda e: e.tensor_copy(dst_buf.v(dst_off, R * Cn), ps(pb, R * Cn)),
                  reads=psk(pb), writes=dst_buf.keys(dst_off, R * Cn))

        def load_x():
            for blk in range(16):
                sbuf_ = A if blk < 8 else B
                st = (blk % 8) * 1024
                P.add("sp", lambda e, blk=blk, st=st, sbuf_=sbuf_: e.dma_start(
                    out=sbuf_.v(st, 1024, dt=F32), in_=x_d[blk * 128:(blk + 1) * 128, :]),
                    writes=sbuf_.keys(st, 1024, dt=F32), dma=True)
            for blk in range(16):
                sbuf_ = A if blk < 8 else B
                st = (blk % 8) * 1024
                for c4 in range(2):
                    pb = bank()
                    for cc in range(4):
                        c = c4 * 4 + cc
                        P.add("pe", lambda e, pb=pb, cc=cc, c=c, st=st, sbuf_=sbuf_: e.transpose(
                            ps(pb, 128, cc * 128), sbuf_.v(st + c * 128, 128, dt=F32), IDF.v(0, 128)),
                            reads=sbuf_.keys(st + c * 128, 128, dt=F32) + IDF.keys(0, 128), writes=psk(pb))
                    keys = []
                    for cc in range(4):
                        keys += X.keys((c4 * 4 + cc) * S + blk * 128, 128)
                    dst = X.t[:, :].rearrange("p (c t) -> p c t", c=KC)[:, c4 * 4:(c4 + 1) * 4, blk * 128:(blk + 1) * 128]
                    src = PS[:, pb, :].rearrange("p (c t) -> p c t", c=4)
                    P.add("act", lambda e, dst=dst, src=src: e.copy(dst, src), reads=psk(pb), writes=keys)

        def store_y(src_buf):
            def stage(blk):
                sl = blk % 6
                if sl < 2:
                    return (lambda off, n: T.v(sl * 1024 + off, n)), (lambda off, n: T.keys(sl * 1024 + off, n))
                return ((lambda off, n: C.v((sl - 2) * 1024 + off, n, dt=F32)),
                        (lambda off, n: C.keys((sl - 2) * 1024 + off, n, dt=F32)))
            for blk in range(16):
                sv, sk = stage(blk)
                for c4 in range(2):
                    pb = bank()
                    for cc in range(4):
                        c = c4 * 4 + cc
                        P.add("pe", lambda e, pb=pb, cc=cc, c=c, blk=blk: e.transpose(
                            ps(pb, 128, cc * 128), src_buf.v(c * S + blk * 128, 128), IDF.v(0, 128)),
                            reads=src_buf.keys(c * S + blk * 128, 128) + IDF.keys(0, 128), writes=psk(pb))
                    if c4 == 0:
                        P.add("act", lambda e, pb=pb, sv=sv: e.copy(sv(0, 512), ps(pb)),
                              reads=psk(pb), writes=sk(0, 512))
                    else:
                        P.add("dve", lambda e, pb=pb, sv=sv: e.tensor_copy(sv(512, 512), ps(pb)),
                              reads=psk(pb), writes=sk(512, 512))
                P.add("sp", lambda e, blk=blk, sv=sv: e.dma_start(
                    out=y_d[blk * 128:(blk + 1) * 128, :], in_=sv(0, 1024)),
                    reads=sk(0, 1024), writes=[("Y", blk)], dma=True)
            P.add("sp", None, reads=[("Y", b) for b in range(16)])

        def rms_stats(t, src_buf, nchunks, stride, inv_n):
            pb = bank()
            for c in range(nchunks):
                q = (c % 2) * TT
                P.add("act", lambda e, c=c, q=q: e.activation(Q.v(q, TT), src_buf.v(c * stride + t * TT, TT), AF.Square),
                      reads=src_buf.keys(c * stride + t * TT, TT), writes=Q.keys(q, TT))
                P.add("pe", lambda e, c=c, q=q: e.matmul(ps(pb), ONES.v(0, 128), Q.v(q, TT),
                                                         start=(c == 0), stop=(c == nchunks - 1)),
                      reads=Q.keys(q, TT) + ONES.keys(0, 128), writes=psk(pb))
            o1 = 4 * TT
            o2 = 4 * TT
            P.add("act", lambda e: e.activation(T.v(o1, TT), ps(pb), AF.Ln, bias=EPSB.v(0, 1), scale=inv_n),
                  reads=psk(pb) + EPSB.keys(0, 1), writes=T.keys(o1, TT))
            P.add("act", lambda e: e.activation(T.v(o2, TT), T.v(o1, TT), AF.Exp, scale=-0.5),
                  reads=T.keys(o1, TT), writes=T.keys(o2, TT))
            return o2

        norm_state = {"grow": None, "inplace": False, "done": set()}

        def norm_tile(grow, t, inplace):
            dstb = X if inplace else H
            o2 = rms_stats(t, X, KC, S, 1.0 / D)
            for c in range(KC):
                P.add("dve", lambda e, c=c, t=t: e.scalar_tensor_tensor(
                    dstb.v(c * S + t * TT, TT), X.v(c * S + t * TT, TT), GV.v(grow * 8 + c, 1),
                    T.v(o2, TT), ALU.mult, ALU.mult),
                    reads=X.keys(c * S + t * TT, TT) + GV.keys(grow * 8 + c, 1) + T.keys(o2, TT),
                    writes=dstb.keys(c * S + t * TT, TT))

        def norm_prepare(grow, inplace=False):
            norm_state["grow"] = grow
            norm_state["inplace"] = inplace
            norm_state["done"] = set()

        def norm_epilogue(t):
            if norm_state["grow"] is not None and t not in norm_state["done"]:
                norm_state["done"].add(t)
                norm_tile(norm_state["grow"], t, norm_state["inplace"])

        def rmsnorm_to_H(grow):
            if norm_state["grow"] != grow or norm_state["inplace"]:
                norm_prepare(grow, False)
            for t in range(NT):
                norm_epilogue(t)
            norm_state["grow"] = None

        def final_rmsnorm_inplace(grow):
            if norm_state["grow"] != grow or not norm_state["inplace"]:
                norm_prepare(grow, True)
            for t in range(NT):
                norm_epilogue(t)
            norm_state["grow"] = None

        EPSB = Buf(nc, es, "EPSB", 8, F32)

        def wload(dst_ap, dst_keys, src_ap):
            P.add("pool", lambda e: e.dma_start(out=dst_ap, in_=src_ap), writes=dst_keys, dma=True)

        hook = [None]
        stepq = []
        step_issued = [False]

        def tick():
            if not stepq:
                return
            if not step_issued[0]:
                stepq[0][0]()
                step_issued[0] = True
            else:
                stepq[0][1]()
                stepq.pop(0)
                step_issued[0] = False
                if stepq:
                    stepq[0][0]()
                    step_issued[0] = True

        def drain():
            while stepq:
                tick()

        def run_hook():
            if hook[0] is not None:
                f = hook[0]
                hook[0] = None
                f()
                tick()

        def rows_T_step(dst_buf, dst_off, rows, Cn):
            R = len(rows)
            assert R * 128 <= 512 and R * Cn <= TT

            def issue():
                for r, row in enumerate(rows):
                    P.add("sp", lambda e, r=r, row=row: e.dma_start(
                        out=VS.v(r * 128, 128, 0, Cn), in_=row.rearrange("(c p) -> c p", p=128)),
                        writes=VS.keys(r * 128, 128), dma=True)

            def consume():
                pb = bank()
                for r in range(R):
                    P.add("pe", lambda e, r=r: e.transpose(ps(pb, Cn, r * Cn), VS.v(r * 128, 128, 0, Cn),
                                                           IDF.v(0, Cn, 0, Cn)),
                          reads=VS.keys(r * 128, 128) + IDF.keys(0, 128), writes=psk(pb))
                P.add("dve", lambda e: e.tensor_copy(dst_buf.v(dst_off, R * Cn), ps(pb, R * Cn)),
                      reads=psk(pb), writes=dst_buf.keys(dst_off, R * Cn))
            stepq.append((issue, consume))

        def pre_ffn(l):
            rows = [dr["ffn_dw_w"][l, 0], dr["ffn_dw_w"][l, 1], dr["ffn_dw_w"][l, 2], dr["ffn_dw_b"][l]]
            rows_T_step(FD, 0, rows, 44)

        def ffn(l):
            w_up = dr["ffn_w_up"][l]
            w_dn = dr["ffn_w_down"][l]
            rmsnorm_to_H(4 + l)
            if ffn_norm_announce[0] is not None:
                norm_prepare(*ffn_norm_announce[0])
            run_hook()
            UB = 1536
            WN = 1024
            groups = [(0, 7), (7, 14), (14, 22)]
            it = [0]
            pending = []
            down_q = []

            def emit_down(gi_, c0_, c1_, wd_off_):
                G_ = c1_ - c0_
                bank_ctr[0] = (it[0] % 2) * 4
                wd = A.v3(wd_off_, G_, D)
                for t in range(NT):
                    for n in range(KC):
                        pb = bank()
                        for j_ in range(G_):
                            sl_ = (c0_ + j_) % 8
                            P.add("pe", lambda e, pb=pb, j_=j_, sl_=sl_, n=n, t=t, wd=wd: e.matmul(
                                ps(pb), wd[:, j_, n * 128:(n + 1) * 128], B.v(sl_ * S + t * TT, TT),
                                start=(j_ == 0), stop=(j_ == G_ - 1)),
                                reads=A.keys(wd_off_ + j_ * D, D) + B.keys(sl_ * S + t * TT, TT), writes=psk(pb))
                        P.add("dve", lambda e, pb=pb, n=n, t=t: e.tensor_tensor(
                            X.v(n * S + t * TT, TT), ps(pb), X.v(n * S + t * TT, TT), ALU.add),
                            reads=psk(pb) + X.keys(n * S + t * TT, TT), writes=X.keys(n * S + t * TT, TT))
                    if gi_ == len(groups) - 1 and t >= 1:
                        norm_epilogue(t - 1)

            def flush_pending():
                while pending:
                    (i, j, tp) = pending.pop(0)
                    ygo = 4 * UB + i * WN
                    yao = i * WN
                    P.add("act", lambda e, ygo=ygo: e.activation(C.v(ygo, WN), C.v(ygo, WN), AF.Silu),
                          reads=C.keys(ygo, WN), writes=C.keys(ygo, WN))
                    P.add("dve", lambda e, ygo=ygo, yao=yao, j=j, tp=tp: e.tensor_tensor(
                        B.v(j * S + tp * WN, WN), T.v(yao, WN), C.v(ygo, WN), ALU.mult),
                        reads=T.keys(yao, WN) + C.keys(ygo, WN), writes=B.keys(j * S + tp * WN, WN))

            for gi, (c0, c1) in enumerate(groups):
                G = c1 - c0
                wd_off = (gi % 2) * 8192
                wload(A.v3(wd_off, G, D), A.keys(wd_off, G * D),
                      w_dn[c0 * 128:c1 * 128, :].rearrange("(j p) n -> p j n", p=128))
                for c in range(c0, c1):
                    j = c % 8
                    so = (c % 3) * 2048
                    slot = W.v3(so, KC, 256)
                    wload(slot[:, :, 0:128], W.keys(so, 2048),
                          w_up[:, c * 128:(c + 1) * 128].rearrange("(k p) n -> p k n", p=128))
                    wload(slot[:, :, 128:256], W.keys(so, 2048),
                          w_up[:, FH + c * 128:FH + (c + 1) * 128].rearrange("(k p) n -> p k n", p=128))
                    if c == c0 + 1 and down_q:
                        emit_down(*down_q.pop(0))
                    for tp in range(2):
                        i = it[0] % 2
                        it[0] += 1
                        base = i * 4
                        uoffs = ((2 * i) * UB, (2 * i + 1) * UB)
                        unext = ((2 * (1 - i)) * UB, (2 * (1 - i) + 1) * UB)
                        yao = i * WN
                        ygo = 4 * UB + i * WN
                        for (bo, co) in ((0, 0), (2, 128)):
                            for half in range(2):
                                t = tp * 2 + half
                                pb = base + bo + half
                                for k in range(KC):
                                    P.add("pe", lambda e, pb=pb, co=co, k=k, t=t, slot=slot: e.matmul(
                                        ps(pb), slot[:, k, co:co + 128], H.v(k * S + t * TT, TT),
                                        start=(k == 0), stop=(k == KC - 1)),
                                        reads=W.keys(so, 2048) + H.keys(k * S + t * TT, TT), writes=psk(pb))
                        for pi, (bo, f) in enumerate(((0, c), (2, FC + c))):
                            uo = uoffs[pi]
                            src = PS[:, base + bo:base + bo + 2, :]
                            pk = psk(base + bo) + psk(base + bo + 1)
                            P.add("act", lambda e, uo=uo, src=src: e.copy(C.v3(uo + 2, 2, TT), src),
                                  reads=pk, writes=C.keys(uo + 2, WN))
                            if pi == 0:
                                P.add("act", lambda e, src=src, f=f, yao=yao: e.activation(
                                    T.v3(yao, 2, TT), src, AF.Identity, bias=FD.v(3 * 44 + f, 1), scale=FD.v(2 * 44 + f, 1)),
                                    reads=pk + FD.keys(0, 176), writes=T.keys(yao, WN))
                            else:
                                P.add("act", lambda e, src=src, f=f, ygo=ygo: e.activation(
                                    C.v3(ygo, 2, TT), src, AF.Identity, bias=FD.v(3 * 44 + f, 1), scale=FD.v(2 * 44 + f, 1)),
                                    reads=pk + FD.keys(0, 176), writes=C.keys(ygo, WN))
                            if tp == 0:
                                un = unext[pi]
                                P.add("act", lambda e, uo=uo, un=un: e.copy(C.v(un, 2), C.v(uo + WN, 2)),
                                      reads=C.keys(uo + WN, 2), writes=C.keys(un, 2))
                        for pi, f in enumerate((c, FC + c)):
                            uo = uoffs[pi]
                            for (tap, sh) in ((1, 1), (0, 2)):
                                lo = sh if tp == 0 else 0
                                nn = WN - lo
                                if pi == 0:
                                    yv = T.v(yao + lo, nn)
                                    yk = T.keys(yao, WN)
                                else:
                                    yv = C.v(ygo + lo, nn)
                                    yk = C.keys(ygo, WN)
                                P.add("dve", lambda e, uo=uo, yv=yv, f=f, tap=tap, sh=sh, lo=lo, nn=nn: e.scalar_tensor_tensor(
                                    yv, C.v(uo + 2 + lo - sh, nn), FD.v(tap * 44 + f, 1), yv, ALU.mult, ALU.add),
                                    reads=C.keys(uo + 2 + lo - sh, nn) + FD.keys(0, 176) + yk, writes=yk)
                        flush_pending()
                        pending.append((i, j, tp))
                    tick()
                down_q.append((gi, c0, c1, wd_off))
            flush_pending()
            while down_q:
                emit_down(*down_q.pop(0))

        def proj_fm(slot, so, ncols_off, t, pb):
            for k in range(KC):
                P.add("pe", lambda e, k=k: e.matmul(ps(pb), slot[:, k, ncols_off:ncols_off + 128],
                                                    H.v(k * S + t * TT, TT), start=(k == 0), stop=(k == KC - 1)),
                      reads=W.keys(so, 2048) + H.keys(k * S + t * TT, TT), writes=psk(pb))

        def out_proj(w_out, kin, wbuf, woff, bias_col=None, tiles=(0, 1, 2, 3), load=True):
            wv = wbuf.v3(woff, KC, D)
            if load:
                for hf in range(2):
                    wload(wv[:, hf * 4:(hf + 1) * 4, :], wbuf.keys(woff + hf * 4096, 4096),
                          w_out[hf * 512:(hf + 1) * 512, :].rearrange("(k p) n -> p k n", p=128))
            for t in tiles:
                for n in range(KC):
                    pb = bank()
                    for k in range(KC):
                        kb, ko = kin[k]
                        P.add("pe", lambda e, k=k, kb=kb, ko=ko, t=t, pb=pb, n=n: e.matmul(
                            ps(pb), wv[:, k, n * 128:(n + 1) * 128], kb.v(ko + t * TT, TT), start=(k == 0), stop=(k == KC - 1)),
                            reads=wbuf.keys(woff + k * D, D) + kb.keys(ko + t * TT, TT), writes=psk(pb))
                    xk = X.keys(n * S + t * TT, TT)
                    if bias_col is None:
                        P.add("dve", lambda e, pb=pb, n=n, t=t: e.tensor_tensor(
                            X.v(n * S + t * TT, TT), ps(pb), X.v(n * S + t * TT, TT), ALU.add),
                            reads=psk(pb) + xk, writes=xk)
                    else:
                        P.add("dve", lambda e, pb=pb, n=n, t=t: e.scalar_tensor_tensor(
                            X.v(n * S + t * TT, TT), ps(pb), OV.v(bias_col + n, 1), X.v(n * S + t * TT, TT),
                            ALU.add, ALU.add),
                            reads=psk(pb) + xk + OV.keys(0, 48), writes=xk)
                if t >= 1:
                    norm_epilogue(t - 1)

        def pre_odd(l):
            i = l // 2
            rows_T_step(OV, 0, [dr["conv_b_in"][i, 0:D], dr["conv_b_in"][i, D:2 * D], dr["conv_dw_b"][i],
                                dr["conv_ln_g"][i]], 8)
            rows_T_step(OV, 32, [dr["conv_ln_b"][i], dr["conv_b_out"][i]], 8)
            for half in range(2):
                def issue(half=half):
                    P.add("sp", lambda e: e.dma_start(
                        out=VS.v(0, 512, 0, 31), in_=dr["conv_dw_w"][i, :, half * 512:(half + 1) * 512]),
                        writes=VS.keys(0, 512), dma=True)

                def consume(half=half):
                    pb = bank()
                    for cc in range(4):
                        P.add("pe", lambda e, cc=cc, pb=pb: e.transpose(ps(pb, 31, cc * 31), VS.v(cc * 128, 128, 0, 31),
                                                                        IDF.v(0, 31, 0, 31)),
                              reads=VS.keys(0, 512) + IDF.keys(0, 128), writes=psk(pb))
                    P.add("dve", lambda e, pb=pb: e.tensor_copy(DWT.v(half * 124, 124), ps(pb, 124)),
                          reads=psk(pb), writes=DWT.keys(0, 248))
                stepq.append((issue, consume))

        def odd_mixer(l):
            i = l // 2
            w_in = dr["conv_w_in"][i]
            w_out = dr["conv_w_out"][i]
            rmsnorm_to_H(l)
            if mixer_norm_announce[0] is not None:
                norm_prepare(mixer_norm_announce[0], False)
            run_hook()
            for c in range(KC):
                so = (c % 3) * 2048
                slot = W.v3(so, KC, 256)
                wload(slot[:, :, 0:128], W.keys(so, 2048),
                      w_in[:, c * 128:(c + 1) * 128].rearrange("(k p) n -> p k n", p=128))
                wload(slot[:, :, 128:256], W.keys(so, 2048),
                      w_in[:, D + c * 128:D + (c + 1) * 128].rearrange("(k p) n -> p k n", p=128))
                for t in range(NT):
                    pa = bank()
                    pg = bank()
                    proj_fm(slot, so, 0, t, pa)
                    proj_fm(slot, so, 128, t, pg)
                    sg = (t % 2) * TT
                    P.add("act", lambda e, pg=pg, sg=sg, c=c: e.activation(
                        T.v(sg, TT), ps(pg), AF.Sigmoid, bias=OV.v(8 + c, 1)),
                        reads=psk(pg) + OV.keys(0, 48), writes=T.keys(sg, TT))
                    P.add("dve", lambda e, pa=pa, sg=sg, c=c, t=t: e.scalar_tensor_tensor(
                        A.v(c * S + t * TT, TT), ps(pa), OV.v(c, 1), T.v(sg, TT), ALU.add, ALU.mult),
                        reads=psk(pa) + OV.keys(0, 48) + T.keys(sg, TT), writes=A.keys(c * S + t * TT, TT))
                tick()
            for c in range(KC):
                dgo = (c % 2) * 4096
                dg = C.v3(dgo, 31, 128)
                for k in range(31):
                    P.add("dve", lambda e, k=k, c=c, dg=dg: e.tensor_scalar(
                        dg[:, k, :], IDB.v(0, 128), DWT.v(c * 31 + k, 1), None, ALU.mult),
                        reads=IDB.keys(0, 128) + DWT.keys(0, 248), writes=C.keys(dgo + k * 128, 128))
                for t in range(NT):
                    pb = bank()
                    for kk in range(31):
                        k = 30 - kk
                        sh = 30 - k
                        lo = sh if t == 0 else 0
                        nn = TT - lo
                        P.add("pe", lambda e, k=k, sh=sh, lo=lo, nn=nn, t=t, c=c, pb=pb, dg=dg: e.matmul(
                            ps(pb, nn, lo), dg[:, k, :], A.v(c * S + t * TT + lo - sh, nn),
                            start=(k == 30), stop=(k == 0)),
                            reads=C.keys(dgo + k * 128, 128) + A.keys(c * S + t * TT + lo - sh, nn), writes=psk(pb))
                    P.add("act", lambda e, pb=pb, c=c, t=t: e.activation(
                        B.v(c * S + t * TT, TT), ps(pb), AF.Identity, bias=OV.v(16 + c, 1)),
                        reads=psk(pb) + OV.keys(0, 48), writes=B.keys(c * S + t * TT, TT))
                    P.add("act", lambda e, pb=pb, c=c, t=t: e.activation(
                        H.v(c * S + t * TT, TT), ps(pb), AF.Square, bias=OV.v(16 + c, 1)),
                        reads=psk(pb) + OV.keys(0, 48), writes=H.keys(c * S + t * TT, TT))
                tick()
            out_proj(w_out, [(A, k * S) for k in range(KC)], C, 0, bias_col=40, tiles=(), load=True)
            ln_banks = {}

            reserved_banks.update((0, 1, 2, 3))

            def ln_stats(t):
                p1 = (t % 2) * 2
                p2 = p1 + 1
                for (pb, src) in ((p1, B), (p2, H)):
                    for c in range(KC):
                        P.add("pe", lambda e, pb=pb, src=src, c=c, t=t: e.matmul(
                            ps(pb), ONES.v(0, 128), src.v(c * S + t * TT, TT), start=(c == 0), stop=(c == KC - 1)),
                            reads=ONES.keys(0, 128) + src.keys(c * S + t * TT, TT), writes=psk(pb))
                ln_banks[t] = (p1, p2)

            def ln_ew(t):
                p1, p2 = ln_banks[t]
                m_, v_, nb_, u_ = 0, TT, 2 * TT, 3 * TT
                P.add("act", lambda e, p1=p1: e.activation(T.v(m_, TT), ps(p1), AF.Copy, scale=1.0 / D),
                      reads=psk(p1), writes=T.keys(m_, TT))
                P.add("act", lambda e, p1=p1: e.activation(T.v(v_, TT), ps(p1), AF.Square, scale=1.0 / D),
                      reads=psk(p1), writes=T.keys(v_, TT))
                P.add("dve", lambda e, p2=p2: e.scalar_tensor_tensor(
                    T.v(v_, TT), ps(p2), 1.0 / D, T.v(v_, TT), ALU.mult, ALU.subtract),
                    reads=psk(p2) + T.keys(v_, TT), writes=T.keys(v_, TT))
                P.add("act", lambda e: e.activation(T.v(v_, TT), T.v(v_, TT), AF.Ln, bias=EPSB.v(0, 1)),
                      reads=T.keys(v_, TT) + EPSB.keys(0, 1), writes=T.keys(v_, TT))
                P.add("act", lambda e: e.activation(T.v(v_, TT), T.v(v_, TT), AF.Exp, scale=-0.5),
                      reads=T.keys(v_, TT), writes=T.keys(v_, TT))
                P.add("dve", lambda e: e.scalar_tensor_tensor(
                    T.v(nb_, TT), T.v(m_, TT), -1.0, T.v(v_, TT), ALU.mult, ALU.mult),
                    reads=T.keys(m_, TT) + T.keys(v_, TT), writes=T.keys(nb_, TT))
                for c in range(KC):
                    uo = u_ + (c % 2) * TT
                    P.add("dve", lambda e, c=c, t=t, uo=uo: e.tensor_tensor(
                        T.v(uo, TT), B.v(c * S + t * TT, TT), T.v(v_, TT), ALU.mult),
                        reads=B.keys(c * S + t * TT, TT) + T.keys(v_, TT), writes=T.keys(uo, TT))
                    P.add("dve", lambda e, uo=uo: e.tensor_tensor(
                        T.v(uo, TT), T.v(uo, TT), T.v(nb_, TT), ALU.add),
                        reads=T.keys(uo, TT) + T.keys(nb_, TT), writes=T.keys(uo, TT))
                    P.add("act", lambda e, c=c, t=t, uo=uo: e.activation(
                        A.v(c * S + t * TT, TT), T.v(uo, TT), AF.Silu, bias=OV.v(32 + c, 1), scale=OV.v(24 + c, 1)),
                        reads=T.keys(uo, TT) + OV.keys(0, 48), writes=A.keys(c * S + t * TT, TT))

            ln_stats(0)
            ln_stats(1)
            ln_ew(0)
            for t in range(NT):
                if t + 2 < NT:
                    ln_stats(t + 2)
                if t + 1 < NT:
                    ln_ew(t + 1)
                out_proj(w_out, [(A, k * S) for k in range(KC)], C, 0, bias_col=40, tiles=(t,), load=False)
            reserved_banks.clear()


        def gen_tables():
            CF = _View(C, 0, F32)
            PI = math.pi
            MAGIC = 12582912.0
            C1 = 6.28125
            C2 = 2.0 * PI - 6.28125
            PIC = 3.1415925
            P.add("dve", lambda e: e.tensor_scalar(SM.v(6, 1), SM.v(3, 1), 1.0 / 32.0, -15.5 / 32.0, ALU.mult, ALU.add),
                  reads=SM.keys(3, 1), writes=SM.keys(6, 1))
            P.add("dve", lambda e: e.tensor_scalar(SM.v(6, 1), SM.v(6, 1), MAGIC, None, ALU.add),
                  reads=SM.keys(6, 1), writes=SM.keys(6, 1))
            P.add("dve", lambda e: e.tensor_scalar(SM.v(7, 1), SM.v(6, 1), -MAGIC, None, ALU.add),
                  reads=SM.keys(6, 1), writes=SM.keys(7, 1))
            P.add("dve", lambda e: e.scalar_tensor_tensor(SM.v(12, 1), SM.v(7, 1), -32.0, SM.v(3, 1), ALU.mult, ALU.add),
                  reads=SM.keys(7, 1) + SM.keys(3, 1), writes=SM.keys(12, 1))
            P.add("act", lambda e: e.activation(SM.v(4, 1), SM.v(12, 1), AF.Exp, scale=-math.log(10000.0) / 32.0),
                  reads=SM.keys(12, 1), writes=SM.keys(4, 1))
            NW = 2048
            for (dst, phase) in ((0, PI / 2.0), (2048, 0.0)):
                ck = CF.keys(dst, NW)
                tk = A.keys(0, NW, dt=F32)
                pos_a = IOTC.v(0, 128).unsqueeze(1).to_broadcast([128, 16, 128])
                pos_b = SM.v(16, 16).unsqueeze(2).to_broadcast([128, 16, 128])
                P.add("dve", lambda e, dst=dst, pos_a=pos_a, pos_b=pos_b: e.tensor_tensor(
                    CF.v(dst, NW).rearrange("p (b j) -> p b j", b=16), pos_a, pos_b, ALU.add),
                    reads=IOTC.keys(0, 128) + SM.keys(16, 16), writes=ck)
                P.add("dve", lambda e, dst=dst, phase=phase: e.tensor_scalar(
                    CF.v(dst, NW), CF.v(dst, NW), SM.v(4, 1), phase, ALU.mult, ALU.add),
                    reads=ck + SM.keys(4, 1), writes=ck)
                P.add("dve", lambda e, dst=dst: e.tensor_scalar(A.v(0, NW, dt=F32), CF.v(dst, NW), 1.0 / (2.0 * PI), MAGIC,
                                                                ALU.mult, ALU.add), reads=ck, writes=tk)
                P.add("dve", lambda e: e.tensor_scalar(A.v(0, NW, dt=F32), A.v(0, NW, dt=F32), -MAGIC, None, ALU.add),
                      reads=tk, writes=tk)
                for cc in (-C1, -C2):
                    P.add("dve", lambda e, dst=dst, cc=cc: e.scalar_tensor_tensor(
                        CF.v(dst, NW), A.v(0, NW, dt=F32), cc, CF.v(dst, NW), ALU.mult, ALU.add),
                        reads=ck + tk, writes=ck)
                P.add("dve", lambda e, dst=dst: e.tensor_scalar(CF.v(dst, NW), CF.v(dst, NW), PIC, -PIC, ALU.min, ALU.max),
                      reads=ck, writes=ck)
                sin_q.append(dst)
            tab_state[0] = True

        sin_q = []

        def gen_tables_finish():
            CF = _View(C, 0, F32)
            while sin_q:
                dst = sin_q.pop(0)
                P.add("act", lambda e, dst=dst: e.activation(CF.v(dst, 2048), CF.v(dst, 2048), SINF),
                      reads=CF.keys(dst, 2048), writes=CF.keys(dst, 2048))
                if not sin_q:
                    P.add("sp", lambda e: e.dma_start(out=tab_d, in_=CF.v(0, 4096)), reads=CF.keys(0, 4096),
                          writes=[("TAB", 0)], dma=True)

        def pre_even(l):
            i = l // 2
            linit = 0.8 - 0.6 * math.exp(-0.3 * l)
            for qi, (na, nb) in enumerate((("diff_lq1", "diff_lk1"), ("diff_lq2", "diff_lk2"))):
                def issue(na=na, nb=nb):
                    P.add("sp", lambda e: e.dma_start(out=VS.v(0, 64), in_=dr[na][i].partition_broadcast(128)),
                          writes=VS.keys(0, 64), dma=True)
                    P.add("sp", lambda e: e.dma_start(out=VS.v(64, 64), in_=dr[nb][i].partition_broadcast(128)),
                          writes=VS.keys(64, 64), dma=True)

                def consume(qi=qi):
                    P.add("dve", lambda e: e.tensor_tensor(VS.v(128, 64), VS.v(0, 64), VS.v(64, 64), ALU.mult),
                          reads=VS.keys(0, 128), writes=VS.keys(128, 64))
                    P.add("dve", lambda e: e.reduce_sum(SM.v(8 + qi, 1), VS.v(128, 64), mybir.AxisListType.X),
                          reads=VS.keys(128, 64), writes=SM.keys(8 + qi, 1))
                    P.add("act", lambda e: e.activation(SM.v(10 + qi, 1), SM.v(8 + qi, 1), AF.Exp),
                          reads=SM.keys(8 + qi, 1), writes=SM.keys(10 + qi, 1))
                stepq.append((issue, consume))

            def issue3():
                P.add("sp", lambda e: e.dma_start(out=SM.v(2, 1), in_=dr["diff_subln_g"][i].rearrange("(p o) -> p o", o=1)),
                      writes=SM.keys(2, 1), dma=True)

            def consume3():
                P.add("dve", lambda e: e.scalar_tensor_tensor(SM.v(0, 1), SM.v(11, 1), -linit, SM.v(10, 1),
                                                              ALU.add, ALU.subtract),
                      reads=SM.keys(10, 2), writes=SM.keys(0, 1))
                P.add("dve", lambda e: e.tensor_scalar(SM.v(1, 1), SM.v(2, 1), 1.0 - linit, None, ALU.mult),
                      reads=SM.keys(2, 1), writes=SM.keys(1, 1))
            stepq.append((issue3, consume3))

        def even_mixer(l):
            i = l // 2
            linit = 0.8 - 0.6 * math.exp(-0.3 * l)
            w_in = dr["ab_w_in"][i]
            w_out = dr["ab_w_out"][i]
            CF = _View(C, 0, F32)
            if STAGE < 1:
                return
            rmsnorm_to_H(l)
            if mixer_norm_announce[0] is not None:
                norm_prepare(mixer_norm_announce[0], False)
            run_hook()
            if not tab_gen[0]:
                gen_tables()
                tab_gen[0] = True
            elif not tab_state[0]:
                P.add("sp", lambda e: e.dma_start(out=CF.v(0, 4096), in_=tab_d), reads=[("TAB", 0)],
                      writes=CF.keys(0, 4096), dma=True)
            tab_state[0] = False
            if STAGE < 3:
                return
            def tok_major(col0, consume):
                for half in range(2):
                    so = ((half + 1) % 3) * 2048
                    slot = W.v3(so, KC, 256)
                    wload(slot, W.keys(so, 2048),
                          w_in[:, col0 + half * 256:col0 + (half + 1) * 256].rearrange("(k p) n -> p k n", p=128))
                for blk in range(16):
                    pb = bank()
                    for half in range(2):
                        so = ((half + 1) % 3) * 2048
                        slot = W.v3(so, KC, 256)
                        for k in range(KC):
                            P.add("pe", lambda e, k=k, blk=blk, pb=pb, half=half, slot=slot: e.matmul(
                                ps(pb, 256, half * 256), H.v(k * S + blk * 128, 128), slot[:, k, :],
                                start=(k == 0), stop=(k == KC - 1)),
                                reads=W.keys(so, 2048) + H.keys(k * S + blk * 128, 128), writes=psk(pb))
                    consume(blk, pb)

            def v_consume(blk, pb):
                P.add("act", lambda e: e.copy(B.v(blk * 512, 512), ps(pb)),
                      reads=psk(pb), writes=B.keys(blk * 512, 512))
            tok_major(1024, v_consume)
            for g in range(4):
                so = (g % 3) * 2048
                slot = W.v3(so, KC, 256)
                wload(slot[:, :, 0:128], W.keys(so, 2048),
                      w_in[:, 1536 + g * 128:1536 + (g + 1) * 128].rearrange("(k p) n -> p k n", p=128))
                for t in range(NT):
                    pb = bank()
                    proj_fm(slot, so, 0, t, pb)
                    P.add("act", lambda e, pb=pb, g=g, t=t: e.activation(
                        B.v(8192 + g * S + t * TT, TT), ps(pb), AF.Gelu_apprx_tanh),
                        reads=psk(pb), writes=B.keys(8192 + g * S + t * TT, TT))
            gen_tables_finish()
            if STAGE < 2:
                return
            rope_pending = []

            def rope_flush():
                while rope_pending:
                    (qc, t, pb, rq, ri) = rope_pending.pop(0)
                    pr = bank()
                    P.add("pe", lambda e, pr=pr, rq=rq: e.matmul(ps(pr), RM.v(0, 128), Q.v(rq, TT), start=True, stop=True),
                          reads=RM.keys(0, 128) + Q.keys(rq, TT), writes=psk(pr))
                    t1 = ((ri % 2) * 2) * TT
                    t2 = t1 + TT
                    P.add("dve", lambda e, pb=pb, t1=t1, t=t: e.tensor_tensor(T.v(t1, TT), ps(pb), CF.v(t * TT, TT), ALU.mult),
                          reads=psk(pb) + CF.keys(t * TT, TT), writes=T.keys(t1, TT))
                    P.add("dve", lambda e, pr=pr, t2=t2, t=t: e.tensor_tensor(T.v(t2, TT), ps(pr), CF.v(2048 + t * TT, TT), ALU.mult),
                          reads=psk(pr) + CF.keys(2048 + t * TT, TT), writes=T.keys(t2, TT))
                    P.add("dve", lambda e, t1=t1, t2=t2, qc=qc, t=t: e.tensor_tensor(
                        A.v(qc * S + t * TT, TT), T.v(t1, TT), T.v(t2, TT), ALU.add),
                        reads=T.keys(t1, TT) + T.keys(t2, TT), writes=A.keys(qc * S + t * TT, TT))

            ri = 0
            for qc in range(8):
                so = (qc % 3) * 2048
                slot = W.v3(so, KC, 256)
                wload(slot[:, :, 0:128], W.keys(so, 2048),
                      w_in[:, qc * 128:(qc + 1) * 128].rearrange("(k p) n -> p k n", p=128))
                for t in range(NT):
                    pb = bank()
                    proj_fm(slot, so, 0, t, pb)
                    rq = (ri % 2) * TT
                    P.add("act", lambda e, pb=pb, rq=rq: e.copy(Q.v(rq, TT), ps(pb)),
                          reads=psk(pb), writes=Q.keys(rq, TT))
                    rope_flush()
                    rope_pending.append((qc, t, pb, rq, ri))
                    ri += 1
                tick()
            rope_flush()
            ZS = 4 * TT
            X_ = mybir.AxisListType.X

            def z_consume(blk, pb):
                zt = (blk % 2) * TT
                sqt = 2 * TT + (blk % 2) * TT
                P.add("act", lambda e: e.activation(T.v(zt, TT), ps(pb), AF.Gelu_apprx_tanh),
                      reads=psk(pb), writes=T.keys(zt, TT))
                P.add("act", lambda e: e.copy(C.v(blk * 512, 512), T.v(zt, TT)),
                      reads=T.keys(zt, TT), writes=C.keys(blk * 512, 512))
                P.add("dve", lambda e: e.tensor_reduce(T.v(ZS + blk * 4, 4), T.v3(zt, 4, 128), X_, ALU.add),
                      reads=T.keys(zt, TT), writes=T.keys(ZS, TT))
                P.add("act", lambda e: e.activation(T.v(sqt, TT), T.v(zt, TT), AF.Square),
                      reads=T.keys(zt, TT), writes=T.keys(sqt, TT))
                P.add("dve", lambda e: e.tensor_reduce(T.v(ZS + 64 + blk * 4, 4), T.v3(sqt, 4, 128), X_, ALU.add),
                      reads=T.keys(sqt, TT), writes=T.keys(ZS, TT))
            tok_major(2048, z_consume)
            zk = T.keys(ZS, TT)
            P.add("dve", lambda e: e.tensor_scalar(T.v(ZS + 128, 64), T.v(ZS, 64), 1.0 / 128.0, None, ALU.mult),
                  reads=zk, writes=zk)
            P.add("dve", lambda e: e.tensor_tensor(T.v(ZS + 256, 64), T.v(ZS + 128, 64), T.v(ZS + 128, 64), ALU.mult),
                  reads=zk, writes=zk)
            P.add("dve", lambda e: e.scalar_tensor_tensor(T.v(ZS + 192, 64), T.v(ZS + 64, 64), 1.0 / 128.0, T.v(ZS + 256, 64),
                                                          ALU.mult, ALU.subtract),
                  reads=zk, writes=zk)
            P.add("act", lambda e: e.activation(T.v(ZS + 192, 64), T.v(ZS + 192, 64), AF.Ln, bias=EPSB.v(0, 1)),
                  reads=zk + EPSB.keys(0, 1), writes=zk)
            P.add("act", lambda e: e.activation(T.v(ZS + 192, 64), T.v(ZS + 192, 64), AF.Exp, scale=-0.5),
                  reads=zk, writes=zk)
            LNG, LNB = 4 * TT, 5 * TT
            P.add("pool", lambda e: e.dma_start(out=T.v(LNG, TT, dt=BF16), in_=dr["sg_ln_g"][i].partition_broadcast(128)),
                  writes=T.keys(LNG, TT, dt=BF16), dma=True)
            P.add("pool", lambda e: e.dma_start(out=T.v(LNB, TT, dt=BF16), in_=dr["sg_ln_b"][i].partition_broadcast(128)),
                  writes=T.keys(LNB, TT, dt=BF16), dma=True)
            z2_list = []

            def z_pass2(blk):
                zt = 3 * TT
                mean_b = T.v(ZS + 128 + blk * 4, 4).unsqueeze(2).to_broadcast([128, 4, 128])
                rstd_b = T.v(ZS + 192 + blk * 4, 4).unsqueeze(2).to_broadcast([128, 4, 128])
                P.add("dve", lambda e, blk=blk, zt=zt, mean_b=mean_b: e.tensor_tensor(
                    T.v3(zt, 4, 128), C.v3(blk * 512, 4, 128), mean_b, ALU.subtract),
                    reads=C.keys(blk * 512, 512) + zk, writes=T.keys(zt, TT))
                P.add("dve", lambda e, zt=zt, rstd_b=rstd_b: e.tensor_tensor(
                    T.v3(zt, 4, 128), T.v3(zt, 4, 128), rstd_b, ALU.mult),
                    reads=T.keys(zt, TT) + zk, writes=T.keys(zt, TT))
                P.add("dve", lambda e, zt=zt: e.tensor_tensor(T.v(zt, TT), T.v(zt, TT), T.v(LNG, TT, dt=BF16), ALU.mult),
                      reads=T.keys(zt, TT) + T.keys(LNG, TT, dt=BF16), writes=T.keys(zt, TT))
                P.add("dve", lambda e, zt=zt, blk=blk: e.tensor_tensor(C.v(blk * 512, 512), T.v(zt, TT), T.v(LNB, TT, dt=BF16), ALU.add),
                      reads=T.keys(zt, TT) + T.keys(LNB, TT, dt=BF16), writes=C.keys(blk * 512, 512))
            z2_list.extend(range(16))
            if STAGE < 4:
                return
            scale = 64.0 ** -0.5
            tasks = []
            for h in range(4):
                for qt in range(NT):
                    for kb in range(4 * qt + 4):
                        tasks.append((h, qt, kb))

            def geom(task):
                h, qt, kb = task
                di = kb - 4 * qt
                lo = 128 * di if di > 0 else 0
                return h, qt, kb, di, lo, TT - lo, 4 * qt + 4

            def emit_qk(idx):
                h, qt, kb, di, lo, nn, nkb = geom(tasks[idx])
                sbk = 4 + 2 * (idx % 2)
                for c in range(2):
                    P.add("pe", lambda e, c=c: e.matmul(
                        ps(sbk + c, nn, lo), A.v((4 + h) * S + kb * 128, 128, 64 * c, 64 * c + 64),
                        A.v(h * S + qt * TT + lo, nn, 64 * c, 64 * c + 64), start=True, stop=(di < 0)),
                        reads=A.keys((4 + h) * S + kb * 128, 128) + A.keys(h * S + qt * TT + lo, nn),
                        writes=psk(sbk + c))
                    if di >= 0:
                        for (so_, mo_, last) in ((0, 256, False), (128, 384, True)):
                            P.add("pe", lambda e, c=c, so_=so_, mo_=mo_, last=last: e.matmul(
                                ps(sbk + c, 128, lo), CM.v(so_, 128, 64 * c, 64 * c + 64), CM.v(mo_, 128, 64 * c, 64 * c + 64),
                                start=False, stop=last),
                                reads=CM.keys(0, 512), writes=psk(sbk + c))

            def emit_rest(idx):
                h, qt, kb, di, lo, nn, nkb = geom(tasks[idx])
                sbk = 4 + 2 * (idx % 2)
                pp = ((2 * idx) % 6) * TT
                pos = [pp, pp + TT]
                P.add("act", lambda e: e.activation(
                    H.v3(pp, 2, TT)[:, :, lo:TT], PS[:, sbk:sbk + 2, lo:TT], AF.Exp, scale=scale),
                    reads=psk(sbk) + psk(sbk + 1), writes=H.keys(pp, 2 * TT))
                for c in range(2):
                    po = pos[c]
                    P.add("pe", lambda e, c=c, po=po: e.matmul(
                        ps(c, nn, lo), B.v(kb * 512 + h * 128, 128), H.v(po + lo, nn),
                        start=(kb == 0), stop=(kb == nkb - 1)),
                        reads=B.keys(kb * 512 + h * 128, 128) + H.keys(po, TT), writes=psk(c))
                    P.add("pe", lambda e, c=c, po=po: e.matmul(
                        ps(2 + c, nn, lo), ONES.v(0, 128), H.v(po + lo, nn),
                        start=(kb == 0), stop=(kb == nkb - 1)),
                        reads=ONES.keys(0, 128) + H.keys(po, TT), writes=psk(2 + c))
                if kb == nkb - 1:
                    fb = 2048 + ((h * NT + qt) % 2) * 2048
                    r1, r2, o1, o2 = fb, fb + TT, fb + 2 * TT, fb + 3 * TT
                    for (rr, bk) in ((r1, 2), (r2, 3)):
                        P.add("act", lambda e, rr=rr, bk=bk: e.activation(H.v(rr, TT, dt=F32), ps(bk), AF.Ln),
                              reads=psk(bk), writes=H.keys(rr, TT, dt=F32))
                    for (oo, bk) in ((o1, 0), (o2, 1)):
                        P.add("dve", lambda e, oo=oo, bk=bk: e.tensor_copy(H.v(oo, TT, dt=F32), ps(bk)),
                              reads=psk(bk), writes=H.keys(oo, TT, dt=F32))
                    for rr in (r1, r2):
                        P.add("act", lambda e, rr=rr: e.activation(H.v(rr, TT, dt=F32), H.v(rr, TT, dt=F32), AF.Exp, scale=-1.0),
                              reads=H.keys(rr, TT, dt=F32), writes=H.keys(rr, TT, dt=F32))
                    for (oo, rr) in ((o1, r1), (o2, r2)):
                        P.add("dve", lambda e, oo=oo, rr=rr: e.tensor_tensor(
                            H.v(oo, TT, dt=F32), H.v(oo, TT, dt=F32), H.v(rr, TT, dt=F32), ALU.mult),
                            reads=H.keys(oo, TT, dt=F32) + H.keys(rr, TT, dt=F32), writes=H.keys(oo, TT, dt=F32))
                    P.add("dve", lambda e: e.scalar_tensor_tensor(
                        A.v(h * S + qt * TT, TT), H.v(o2, TT, dt=F32), SM.v(0, 1), H.v(o1, TT, dt=F32), ALU.mult, ALU.add),
                        reads=H.keys(o1, TT, dt=F32) + H.keys(o2, TT, dt=F32) + SM.keys(0, 1), writes=A.keys(h * S + qt * TT, TT))
                    subln_q.append((idx + 3, h, qt))

            def subln(h, qt, cur):
                it = h * NT + qt
                ao = h * S + qt * TT
                sq = (it % 2) * TT
                tr = TT
                P.add("act", lambda e: e.activation(Q.v(sq, TT), A.v(ao, TT), AF.Square),
                      reads=A.keys(ao, TT), writes=Q.keys(sq, TT))
                pss = 4 + 2 * (cur % 2)
                P.add("pe", lambda e: e.matmul(ps(pss), ONES.v(0, 128), Q.v(sq, TT), start=True, stop=True),
                      reads=ONES.keys(0, 128) + Q.keys(sq, TT), writes=psk(pss))
                P.add("act", lambda e: e.activation(T.v(tr, TT), ps(pss), AF.Ln, bias=EPSB.v(0, 1), scale=1.0 / 128.0),
                      reads=psk(pss) + EPSB.keys(0, 1), writes=T.keys(tr, TT))
                P.add("act", lambda e: e.activation(T.v(tr, TT), T.v(tr, TT), AF.Exp, scale=-0.5),
                      reads=T.keys(tr, TT), writes=T.keys(tr, TT))
                P.add("dve", lambda e: e.scalar_tensor_tensor(
                    A.v(ao, TT), A.v(ao, TT), SM.v(1, 1), T.v(tr, TT), ALU.mult, ALU.mult),
                    reads=A.keys(ao, TT) + T.keys(tr, TT) + SM.keys(1, 1), writes=A.keys(ao, TT))

            subln_q = []
            emit_qk(0)
            for idx in range(len(tasks)):
                if idx + 1 < len(tasks):
                    emit_qk(idx + 1)
                emit_rest(idx)
                if z2_list and idx % 8 == 4:
                    z_pass2(z2_list.pop(0))
                if idx % 8 == 0:
                    tick()
                while subln_q and subln_q[0][0] <= idx:
                    _, hh, qq = subln_q.pop(0)
                    subln(hh, qq, idx)
            while z2_list:
                z_pass2(z2_list.pop(0))
            while subln_q:
                _, hh, qq = subln_q.pop(0)
                subln(hh, qq, len(tasks) - 1)
            if STAGE < 5:
                return
            SGW = 3072
            P.add("sp", lambda e: e.dma_start(out=VS.v3(0, 4, 128), in_=dr["sg_w"][i].rearrange("g t s -> t g s")),
                  writes=VS.keys(0, 512), dma=True)
            pb = bank()
            for g in range(4):
                P.add("pe", lambda e, g=g, pb=pb: e.transpose(ps(pb, 128, g * 128), VS.v(g * 128, 128), IDF.v(0, 128)),
                      reads=VS.keys(0, 512) + IDF.keys(0, 128), writes=psk(pb))
            for g in range(4):
                P.add("dve", lambda e, g=g, pb=pb: e.tensor_tensor(H.v(SGW + g * 128, 128), ps(pb, 128, g * 128),
                                                                    MASK.v(0, 128), ALU.mult),
                      reads=psk(pb) + MASK.keys(0, 128), writes=H.keys(SGW + g * 128, 128))
            BT = 4 * TT
            P.add("sp", lambda e: e.dma_start(out=T.v(BT, TT), in_=dr["sg_b"][i].rearrange("g t -> (g t)").partition_broadcast(128)),
                  writes=T.keys(BT, TT), dma=True)
            for g in range(4):
                for tt in range(NT):
                    pb = bank()
                    for cc in range(4):
                        blk = tt * 4 + cc
                        P.add("pe", lambda e, g=g, cc=cc, blk=blk, pb=pb: e.matmul(
                            ps(pb, 128, cc * 128), C.v(blk * 512 + g * 128, 128), H.v(SGW + g * 128, 128),
                            start=True, stop=True),
                            reads=C.keys(blk * 512 + g * 128, 128) + H.keys(SGW + g * 128, 128), writes=psk(pb))
                    tz = ((g * NT + tt) % 2) * TT
                    bt_b = T.v(BT + g * 128, 128).unsqueeze(1).to_broadcast([128, 4, 128])
                    P.add("dve", lambda e, pb=pb, tz=tz, bt_b=bt_b: e.tensor_tensor(
                        T.v3(tz, 4, 128), PS[:, pb, :].rearrange("p (c t) -> p c t", c=4), bt_b, ALU.add),
                        reads=psk(pb) + T.keys(BT, TT), writes=T.keys(tz, TT))
                    uo = 8192 + g * S + tt * TT
                    P.add("dve", lambda e, tz=tz, uo=uo: e.tensor_tensor(B.v(uo, TT), T.v(tz, TT), B.v(uo, TT), ALU.mult),
                          reads=T.keys(tz, TT) + B.keys(uo, TT), writes=B.keys(uo, TT))
            out_proj(w_out, [(A, k * S) for k in range(4)] + [(B, 8192 + g * S) for g in range(4)], B, 0)

        consts()
        P.add("dve", lambda e: e.memset(EPSB.v(0, 8), EPS), writes=EPSB.keys(0, 8))
        load_rows_T(GV, 0, [dr["norm_mix_g"][i] for i in range(4)], 8)
        load_rows_T(GV, 32, [dr["norm_ffn_g"][i] for i in range(4)], 8)
        load_rows_T(GV, 64, [dr["final_norm_g"]], 8)
        load_x()
        mixer_norm_announce = [None]
        ffn_norm_announce = [None]

        def pre_mixer(l):
            if l % 2 == 0:
                pre_even(l)
            else:
                pre_odd(l)

        if do_mixer and len(layers) > 0:
            pre_mixer(layers[0])
            drain()
        for li, l in enumerate(layers):
            if do_mixer:
                if do_ffn:
                    hook[0] = (lambda l=l: pre_ffn(l))
                    norm_after_mixer = 4 + l
                else:
                    norm_after_mixer = None
                mixer_norm_announce[0] = norm_after_mixer
                if l % 2 == 0:
                    even_mixer(l)
                else:
                    odd_mixer(l)
                run_hook()
                drain()
            elif do_ffn:
                pre_ffn(l)
                drain()
            if do_ffn:
                if do_mixer and li + 1 < len(layers):
                    hook[0] = (lambda nl=layers[li + 1]: pre_mixer(nl))
                if li + 1 < len(layers):
                    ffn_norm_announce[0] = (layers[li + 1], False) if do_mixer else (4 + layers[li + 1], False)
                elif final_norm:
                    ffn_norm_announce[0] = (8, True)
                else:
                    ffn_norm_announce[0] = None
                ffn(l)
                run_hook()
                drain()
        if final_norm:
            final_rmsnorm_inplace(8)
        store_y(X)
        P.emit(nc, es)
    return nc


_NAMES = ["norm_mix_g", "norm_ffn_g", "ab_w_in", "ab_w_out", "diff_lq1", "diff_lk1", "diff_lq2",
          "diff_lk2", "diff_subln_g", "sg_ln_g", "sg_ln_b", "sg_w", "sg_b", "conv_w_in", "conv_b_in",
          "conv_dw_w", "conv_dw_b", "conv_ln_g", "conv_ln_b", "conv_w_out", "conv_b_out", "ffn_w_up",
          "ffn_dw_w", "ffn_dw_b", "ffn_w_down", "final_norm_g"]


def run(inputs, ncores=8, **bkw):
    import time
    t0 = time.time()
    nc = build_program(**bkw)
    print("build_program s:", time.time() - t0, flush=True)
    shared = {k: np.ascontiguousarray(np.asarray(inputs[k], dtype=np.float32)) for k in _NAMES}
    x = np.asarray(inputs["x"], dtype=np.float32)
    in_maps = []
    for i in range(ncores):
        m = dict(shared)
        m["x"] = np.ascontiguousarray(x[i])
        in_maps.append(m)
    t0 = time.time()
    res = run_bass_kernel_spmd(nc, in_maps, core_ids=list(range(ncores)))
    print("run_bass_kernel_spmd s:", time.time() - t0, flush=True)
    return np.stack([np.asarray(r["y"], dtype=np.float32) for r in res.results], axis=0)


def kernel(**inputs):
    return run(inputs, ncores=8)
```
